# Optimizing a Trainium2 kernel written in Bass

```python
import jax, jax.numpy as jnp
from jax import lax
import numpy as np

D_MODEL = 2048
BATCH = 8
SEQ = 2048
DEPTH = 1

HEAD_DIM = 128
N_HEADS = D_MODEL // HEAD_DIM
N_HEADS_B = N_HEADS // 4
N_HEADS_A = N_HEADS - N_HEADS_B
DILATED_PAIRS = ((128, 1), (512, 4), (2048, 16))
N_GROUPS_A = len(DILATED_PAIRS)
HEADS_PER_GROUP_A = N_HEADS_A // N_GROUPS_A
GRID_W = 64
WIN_R = 8
WIN_C = 16
QKV_W = N_HEADS * HEAD_DIM
D_IN = 3 * QKV_W + 2 * D_MODEL
D_A_OUT = HEADS_PER_GROUP_A * HEAD_DIM
D_B_OUT = N_HEADS_B * HEAD_DIM
D_FF = 4 * D_MODEL
ROPE_THETA = 10000.0
EPS = 1e-6
NEG_INF = -1e30

kernel_name = "hybrid_dilated_neighbourhood_gated_encoder"


def rms_norm(x, g):
    x32 = x.astype(jnp.float32)
    y = x32 * lax.rsqrt(jnp.mean(x32 * x32, axis=-1, keepdims=True) + EPS)
    return (y * g.astype(jnp.float32)).astype(x.dtype)


def rope(x, seq_len):
    pos = jnp.arange(seq_len, dtype=jnp.float32)
    inv = ROPE_THETA ** (-jnp.arange(0, HEAD_DIM, 2, dtype=jnp.float32) / HEAD_DIM)
    ang = pos[:, None] * inv[None, :]
    cos = jnp.cos(ang)[None, :, None, :]
    sin = jnp.sin(ang)[None, :, None, :]
    x32 = x.astype(jnp.float32)
    x1, x2 = jnp.split(x32, 2, axis=-1)
    return jnp.concatenate([x1 * cos - x2 * sin, x2 * cos + x1 * sin], axis=-1).astype(x.dtype)


def dilated_window_attention(q, k, v, window, dil):
    B, S, H, E = q.shape
    half = window // (2 * dil)
    blk = half
    M = S // dil
    nb = -(-M // blk)
    Mp = nb * blk

    def to_sub(t):
        return t.reshape(B, M, dil, H, E).transpose(0, 2, 3, 1, 4)

    qs = jnp.pad(to_sub(q), ((0, 0), (0, 0), (0, 0), (0, Mp - M), (0, 0)))
    qs = qs.reshape(B, dil, H, nb, blk, E)
    pad_kv = ((0, 0), (0, 0), (0, 0), (half, Mp - M + half), (0, 0))
    ks = jnp.pad(to_sub(k), pad_kv)
    vs = jnp.pad(to_sub(v), pad_kv)
    kb_len = blk + 2 * half
    idx = (jnp.arange(nb) * blk)[:, None] + jnp.arange(kb_len)[None, :]
    kb = ks[:, :, :, idx]
    vb = vs[:, :, :, idx]
    kpos = idx - half
    qpos = (jnp.arange(nb) * blk)[:, None] + jnp.arange(blk)[None, :]
    valid = (kpos >= 0) & (kpos < M)
    mask = (jnp.abs(kpos[:, None, :] - qpos[:, :, None]) <= half) & valid[:, None, :]
    s = jnp.einsum('bdhnqe,bdhnke->bdhnqk', qs, kb,
                   preferred_element_type=jnp.float32) * (E ** -0.5)
    s = jnp.where(mask, s, NEG_INF)
    lse = jax.nn.logsumexp(s, axis=-1)
    p = jnp.exp(s - lse[..., None]).astype(v.dtype)
    o = jnp.einsum('bdhnqk,bdhnke->bdhnqe', p, vb)
    o = o.reshape(B, dil, H, Mp, E)[:, :, :, :M].transpose(0, 3, 1, 2, 4).reshape(B, S, H, E)
    lse = lse.reshape(B, dil, H, Mp)[..., :M].transpose(0, 3, 1, 2).reshape(B, S, H)
    return o, lse


def neighbourhood_attention(q, k, v, rpb):
    B, S, H, E = q.shape
    rows = S // GRID_W
    kr = min(WIN_R, rows)
    kc = WIN_C

    def to_grid(t):
        return t.reshape(B, rows, GRID_W, H, E).transpose(0, 3, 1, 2, 4)

    qg, kg, vg = to_grid(q), to_grid(k), to_grid(v)
    r = jnp.arange(rows)
    row_idx = jnp.clip(r - kr // 2, 0, rows - kr)[:, None] + jnp.arange(kr)[None, :]
    kb = kg[:, :, row_idx]
    vb = vg[:, :, row_idx]
    c = jnp.arange(GRID_W)
    col_start = jnp.clip(c - kc // 2, 0, GRID_W - kc)
    col_mask = (c[None, :] >= col_start[:, None]) & (c[None, :] < col_start[:, None] + kc)
    dr = row_idx - r[:, None] + (WIN_R - 1)
    dc = jnp.clip(c[None, :] - c[:, None], -(kc - 1), kc - 1) + (WIN_C - 1)
    bias = rpb[:, dr[:, None, :, None], dc[None, :, None, :]]
    s = jnp.einsum('bhrqe,bhrjke->bhrqjk', qg, kb,
                   preferred_element_type=jnp.float32) * (E ** -0.5)
    s = s + bias.astype(jnp.float32)[None]
    s = jnp.where(col_mask[:, None, :], s, NEG_INF)
    p = jax.nn.softmax(s.reshape(B, H, rows, GRID_W, kr * GRID_W), axis=-1)
    p = p.reshape(s.shape).astype(v.dtype)
    o = jnp.einsum('bhrqjk,bhrjke->bhrqe', p, vb)
    return o.transpose(0, 2, 3, 1, 4).reshape(B, S, H, E)


def setup_inputs(seed: int = 0) -> dict:
    key = jax.random.key(seed)
    ks = jax.random.split(key, 16)
    f32 = jnp.float32

    def nrm(k, shape, scale):
        return jax.random.normal(k, shape, f32) * scale

    return {
        "x": nrm(ks[0], (BATCH, SEQ, D_MODEL), 1.0),
        "norm_mix": 1.0 + nrm(ks[1], (DEPTH, D_MODEL), 0.05),
        "w_in": nrm(ks[2], (DEPTH, D_MODEL, D_IN), D_MODEL ** -0.5),
        "b_gate": nrm(ks[3], (DEPTH, 2 * D_MODEL), 0.1),
        "q_norm_a": 1.0 + nrm(ks[4], (DEPTH, HEAD_DIM), 0.05),
        "k_norm_a": 1.0 + nrm(ks[5], (DEPTH, HEAD_DIM), 0.05),
        "q_norm_b": 1.0 + nrm(ks[6], (DEPTH, HEAD_DIM), 0.05),
        "k_norm_b": 1.0 + nrm(ks[7], (DEPTH, HEAD_DIM), 0.05),
        "rpb_b": nrm(ks[8], (DEPTH, N_HEADS_B, 2 * WIN_R - 1, 2 * WIN_C - 1), 0.1),
        "w_proj_a": nrm(ks[9], (DEPTH, D_A_OUT, D_MODEL), D_A_OUT ** -0.5),
        "w_proj_b": nrm(ks[10], (DEPTH, D_B_OUT, D_MODEL), D_B_OUT ** -0.5),
        "w_out": nrm(ks[11], (DEPTH, D_MODEL, D_MODEL), D_MODEL ** -0.5),
        "norm_ffn": 1.0 + nrm(ks[12], (DEPTH, D_MODEL), 0.05),
        "w_up": nrm(ks[13], (DEPTH, D_MODEL, D_FF), D_MODEL ** -0.5),
        "w_down": nrm(ks[14], (DEPTH, D_FF, D_MODEL), D_FF ** -0.5),
    }


def reference(x, norm_mix, w_in, b_gate, q_norm_a, k_norm_a, q_norm_b, k_norm_b, rpb_b,
              w_proj_a, w_proj_b, w_out, norm_ffn, w_up, w_down):
    B, S, _ = x.shape
    h = x
    for l in range(DEPTH):
        xn = rms_norm(h, norm_mix[l])
        proj = xn @ w_in[l]
        q, k, v, gate = jnp.split(proj, [QKV_W, 2 * QKV_W, 3 * QKV_W], axis=-1)
        q = q.reshape(B, S, N_HEADS, HEAD_DIM)
        k = k.reshape(B, S, N_HEADS, HEAD_DIM)
        v = v.reshape(B, S, N_HEADS, HEAD_DIM)

        qa = rope(rms_norm(q[:, :, :N_HEADS_A], q_norm_a[l]), S)
        ka = rope(rms_norm(k[:, :, :N_HEADS_A], k_norm_a[l]), S)
        va = v[:, :, :N_HEADS_A]
        outs, lses = [], []
        for g, (win, dil) in enumerate(DILATED_PAIRS):
            sl = slice(g * HEADS_PER_GROUP_A, (g + 1) * HEADS_PER_GROUP_A)
            o_g, lse_g = dilated_window_attention(qa[:, :, sl], ka[:, :, sl], va[:, :, sl], win, dil)
            outs.append(o_g)
            lses.append(lse_g)
        wts = jax.nn.softmax(jnp.stack(lses, axis=0), axis=0)
        oa = jnp.einsum('gbsh,gbshe->bshe', wts,
                        jnp.stack(outs, axis=0).astype(jnp.float32)).astype(x.dtype)

        qb = rms_norm(q[:, :, N_HEADS_A:], q_norm_b[l])
        kb = rms_norm(k[:, :, N_HEADS_A:], k_norm_b[l])
        ob = neighbourhood_attention(qb, kb, v[:, :, N_HEADS_A:], rpb_b[l])

        ya = oa.reshape(B, S, D_A_OUT) @ w_proj_a[l]
        yb = ob.reshape(B, S, D_B_OUT) @ w_proj_b[l]
        ga, gb = jnp.split(jax.nn.sigmoid((gate + b_gate[l]).astype(jnp.float32)), 2, axis=-1)
        mixed = (ga * ya + gb * yb).astype(x.dtype)
        h = h + mixed @ w_out[l]

        hn = rms_norm(h, norm_ffn[l])
        u = jax.nn.relu(hn @ w_up[l])
        h = h + (u * u) @ w_down[l]
    return h
```

```python
import numpy as np
import ml_dtypes
import concourse.bass as bass
import concourse.mybir as mybir
from concourse.bass_utils import run_bass_kernel_spmd

F32 = mybir.dt.float32
BF16 = mybir.dt.bfloat16
AF = mybir.ActivationFunctionType
ALU = mybir.AluOpType

S = 2048
D = 2048
NCORES = 8
EPS = 1e-6
SCALE = 128.0 ** -0.5
NEG = -30000.0
ENGS = ("pe", "act", "dve", "pool", "sp")


class Op:
    __slots__ = ("eng", "fn", "idx", "waits", "signaled", "dma_sem", "dma_val")


class Reg:
    __slots__ = ("arena", "lo", "hi", "dj")

    def __init__(self, arena, lo, hi, dj=None):
        self.arena, self.lo, self.hi, self.dj = arena, lo, hi, dj


class Prog:
    def __init__(self, nc):
        self.nc = nc
        self.q = {e: [] for e in ENGS}
        self.known = {e: {} for e in ENGS}
        self.recs = {}
        self.excl = set()
        self.dsem = {}

    def _overl(self, reg):
        out = []
        for rec in self.recs.get(reg.arena, ()):
            if rec[0] < reg.hi and reg.lo < rec[1]:
                d0 = rec[2]
                if d0 is not None and reg.dj is not None and d0[0] == reg.dj[0] and d0[1] != reg.dj[1]:
                    continue
                out.append(rec)
        return out

    def add(self, eng, fn, reads=(), writes=(), sem=None):
        op = Op()
        op.eng, op.fn, op.idx, op.signaled = eng, fn, len(self.q[eng]), False
        op.dma_sem, op.dma_val = sem, 0
        deps = {}

        def need(o):
            if o is None:
                return
            if o.dma_sem is not None:
                key, val = ("s", o.dma_sem), self.dsem[o.dma_sem]
            else:
                if o.eng == "pe" and eng == "pe":
                    return
                key, val = ("e", o.eng), o.idx
            if deps.get(key, -1) < val:
                deps[key] = val

        racc, wacc = [], []
        for r in reads:
            (wacc if r.arena in self.excl else racc).append(r)
        wacc.extend(writes)
        for r in racc:
            for rec in self._overl(r):
                need(rec[3])
        for r in wacc:
            for rec in self._overl(r):
                need(rec[3])
                for o in rec[4].values():
                    need(o)
        waits = []
        kn = self.known[eng]
        for key, val in deps.items():
            if kn.get(key, -1) >= val:
                continue
            kn[key] = val
            waits.append((key, val))
            if key[0] == "e":
                self.q[key[1]][val].signaled = True
        op.waits = waits
        if sem is not None:
            self.dsem[sem] = self.dsem.get(sem, 0) + 16
            op.dma_val = self.dsem[sem]
        for r in racc:
            ov = self._overl(r)
            contained = False
            for rec in ov:
                rec[4][eng if sem is None else ("d", sem, op.idx)] = op
                if rec[0] <= r.lo and r.hi <= rec[1]:
                    contained = True
            if not contained:
                self.recs.setdefault(r.arena, []).append([r.lo, r.hi, r.dj, None, {(eng if sem is None else ("d", sem, op.idx)): op}])
        for r in wacc:
            lst = self.recs.setdefault(r.arena, [])
            ov = self._overl(r)
            for rec in ov:
                if r.lo <= rec[0] and rec[1] <= r.hi:
                    lst.remove(rec)
            lst.append([r.lo, r.hi, r.dj, op, {}])
        self.q[eng].append(op)
        return op

    def emit(self, esems, dsems):
        tick = {}
        for e in ENGS:
            c = 0
            t = []
            for op in self.q[e]:
                if op.signaled:
                    c += 1
                t.append(c)
            tick[e] = t
        self.tick = tick

        def run(e, eng):
            for op in self.q[e]:
                for key, val in op.waits:
                    if key[0] == "e":
                        eng.wait_ge(esems[key[1]], tick[key[1]][val])
                    else:
                        eng.wait_ge(dsems[key[1]], val)
                ins = op.fn(eng)
                if op.dma_sem is not None:
                    ins.then_inc(dsems[op.dma_sem], 16)
                elif op.signaled:
                    ins.then_inc(esems[e], 1)
        return run


def _const_tables():
    bf = ml_dtypes.bfloat16
    cb = np.zeros((128, 7, 128), np.float32)
    cb[:, 0, :] = 1.0 / 2048.0
    cb[:, 1, :] = 1.0 / 128.0
    cb[:, 2, :] = 1.0
    ein = np.arange(128)[:, None]
    eout = np.arange(128)[None, :]
    cb[:, 3, :] = (ein == (eout + 64) % 128)
    kk = np.arange(128)[:, None]
    qq = np.arange(128)[None, :]
    cb[:, 4, :] = (kk - qq >= 64)
    cb[:, 5, :] = (np.abs(kk - qq) <= 64)
    cb[:, 6, :] = (qq - kk >= 64)
    cb = cb.reshape(128, 7 * 128).astype(bf)
    pos = np.arange(S, dtype=np.float32)
    inv = (np.float32(10000.0) ** (-np.arange(0, 128, 2, dtype=np.float32) / np.float32(128))).astype(np.float32)
    ang = (pos[:, None] * inv[None, :]).astype(np.float32)
    cos = np.cos(ang).astype(np.float32).T
    sin = np.sin(ang).astype(np.float32).T
    cosT = np.concatenate([cos, cos], axis=0)
    sinT = np.concatenate([-sin, sin], axis=0)
    rope = np.ascontiguousarray(np.stack([cosT, sinT], axis=1))
    maskc = np.zeros((128, 2, 14, 64), np.float32)
    cq = np.arange(64)
    cstart = np.clip(cq - 8, 0, 48)
    for jj in range(2):
        for ck in range(64):
            p = jj * 64 + ck
            colv = (ck >= cstart) & (ck < cstart + 16)
            for v in range(2):
                for dd in range(14):
                    e = 13 - dd + jj
                    ok = colv & ((v == 1) or (3 <= e <= 10))
                    maskc[p, v, dd, :] = np.where(ok, 0.0, NEG)
    return cb, rope, maskc.reshape(128, 2 * 14 * 64)


def _rpb_gather(rpb):
    jj = np.arange(2)[:, None, None, None]
    ck = np.arange(64)[None, :, None, None]
    dd = np.arange(14)[None, None, :, None]
    cq = np.arange(64)[None, None, None, :]
    e = np.broadcast_to(13 - dd + jj, (2, 64, 14, 64))
    dc = np.broadcast_to(np.clip(ck - cq, -15, 15) + 15, (2, 64, 14, 64))
    g = rpb[:, e, dc]
    g = g.reshape(4, 128, 1, 14, 64)
    g = np.broadcast_to(g, (4, 128, 2, 14, 64))
    return np.ascontiguousarray(g).reshape(4, 128, 2 * 14 * 64).astype(np.float32)


SB_BYTES = 208896


def build(debug=False):
    nc = bass.Bass("TRN2", target_bir_lowering=False)
    P = Prog(nc)

    def dram(name, shape, dt, kind="ExternalInput"):
        return nc.dram_tensor(name, list(shape), dt, kind=kind).ap()

    xT = dram("xT", [D, S], F32)
    w_in = dram("w_in", [D, 10240], F32)
    w_pa = dram("w_pa", [512, D], F32)
    w_pb = dram("w_pb", [512, D], F32)
    w_out = dram("w_out", [D, D], F32)
    w_up = dram("w_up", [D, 8192], F32)
    w_dn = dram("w_dn", [8192, D], F32)
    cvec = dram("cvec", [128, 69], F32)
    cbd = dram("cb", [128, 896], BF16)
    roped = dram("rope", [128, 2, S], F32)
    maskcd = dram("maskc", [128, 1792], F32)
    rpbg = dram("rpbg", [4, 128, 1792], F32)
    outT = dram("outT", [D, S], F32, kind="ExternalOutput")
    s_out = dram("s_out", [16, 128, 2048], BF16, kind="Internal")
    s_up = dram("s_up", [64, 128, 2048], BF16, kind="Internal")
    s_dn = dram("s_dn", [64, 128, 2048], BF16, kind="Internal")
    if debug:
        dbg_xn = dram("dbg_xn", [128, 16 * S], BF16, kind="ExternalOutput")
        dbg_oab = dram("dbg_oab", [128, 8 * S], BF16, kind="ExternalOutput")
        dbg_mix = dram("dbg_mix", [128, 4 * 16 * 512], BF16, kind="ExternalOutput")

    SBt = nc.alloc_sbuf_tensor("SB", [128, SB_BYTES // 2], BF16).ap()
    CF = nc.alloc_sbuf_tensor("CF", [128, 69], F32).ap()
    CB = nc.alloc_sbuf_tensor("CBt", [128, 896], BF16).ap()
    PS = [nc.alloc_psum_tensor("ps%d" % i, [128, 512], F32).ap() for i in range(8)]
    for i in range(8):
        P.excl.add("PS%d" % i)

    def psr(i):
        return Reg("PS%d" % i, 0, 2048)

    class Buf:
        def __init__(self, off, nbytes, dt):
            self.off, self.nbytes, self.dt = off, nbytes, dt
            assert off % 4 == 0 and off + nbytes <= SB_BYTES, (off, nbytes)
            a = SBt[:, off // 2:(off + nbytes) // 2]
            self.ap = a.bitcast(F32) if dt == F32 else a
            self.esz = 4 if dt == F32 else 2

        def reg(self, lo=None, hi=None, dj=None):
            if lo is None:
                return Reg("SB", self.off, self.off + self.nbytes, dj)
            return Reg("SB", self.off + lo * self.esz, self.off + hi * self.esz, dj)

    def creg(name):
        return Reg(name, 0, 1)

    g1 = lambda k: CF[:, k:k + 1]
    g2 = lambda k: CF[:, 16 + k:17 + k]
    bgc = lambda c: CF[:, 32 + c:33 + c]
    gqa, gka, gqb, gkb = CF[:, 64:65], CF[:, 65:66], CF[:, 66:67], CF[:, 67:68]
    ones2048 = CB[:, 0:128]
    ones128 = CB[:, 128:256]
    ones1 = CB[:, 256:384]
    permT = CB[:, 384:512]
    maskA = CB[:, 512:896]

    epsc = CF[:, 68:69]

    def rstd_ops(dst, pbank):
        P.add("act", lambda e: e.activation(out=dst.ap, in_=PS[pbank][:, :], func=AF.Sqrt, bias=epsc, scale=1.0),
              reads=[psr(pbank), RCF], writes=[dst.reg()])
        P.add("dve", lambda e: e.reciprocal(out=dst.ap, in_=dst.ap), reads=[dst.reg()], writes=[dst.reg()])

    P.add("sp", lambda e: e.dma_start(out=CF[:, :], in_=cvec[:, :]), writes=[creg("CF")], sem="const")
    P.add("sp", lambda e: e.dma_start(out=CB[:, :], in_=cbd[:, :]), writes=[creg("CB")], sem="const")
    RCF, RCB = creg("CF"), creg("CB")

    xTv = xT.rearrange("(k p) t -> p k t", p=128)
    outTv = outT.rearrange("(k p) t -> p k t", p=128)
    w_in_v = w_in.rearrange("(k p) c -> p k c", p=128)

    XN = Buf(0, 65536, BF16)
    XNv = XN.ap.rearrange("p (k t) -> p k t", t=S)
    OAB = Buf(65536, 32768, BF16)
    OABv = OAB.ap.rearrange("p (j t) -> p j t", t=S)
    TB0 = 98304

    def xn_reg(k, t0=0, t1=S):
        return XN.reg(k * S + t0, k * S + t1)

    precast_jobs = []
    w_out_v = w_out.rearrange("(k p) c -> p k c", p=128)
    w_up_v = w_up.rearrange("(k p) c -> p k c", p=128)
    w_dn_v = w_dn.rearrange("(g k p) c -> g p k c", p=128, k=16)
    for c in range(16):
        precast_jobs.append((s_out[c].rearrange("p (k c) -> p k c", c=128), w_out_v[:, :, c * 128:(c + 1) * 128], Reg("s_out", c, c + 1)))
    for f in range(64):
        precast_jobs.append((s_up[f].rearrange("p (k c) -> p k c", c=128), w_up_v[:, :, f * 128:(f + 1) * 128], Reg("s_up", f, f + 1)))
    for c in range(16):
        for fg in range(4):
            t = c * 4 + fg
            precast_jobs.append((s_dn[t].rearrange("p (k c) -> p k c", c=128), w_dn_v[fg][:, :, c * 128:(c + 1) * 128], Reg("s_dn", t, t + 1)))
    precast_pos = [0]

    def emit_precast(n):
        for _ in range(n):
            if precast_pos[0] >= len(precast_jobs):
                return
            dst, src, rg = precast_jobs[precast_pos[0]]
            precast_pos[0] += 1
            P.add("pool", lambda e, dst=dst, src=src: e.dma_start(out=dst, in_=src), writes=[rg], sem="precast")

    XS = [Buf(TB0, 32768, F32), Buf(TB0 + 32768, 32768, F32)]
    SQA = [Buf(TB0 + 65536 + i * 1024, 1024, BF16) for i in range(4)]
    R1 = [Buf(TB0 + 65536 + 4096 + i * 2048, 2048, F32) for i in range(2)]
    for tb in range(4):
        xs = XS[tb % 2]
        xsv = xs.ap.rearrange("p (k t) -> p k t", t=512)
        for j in range(4):
            P.add("sp", lambda e, xsv=xsv, j=j, tb=tb: e.dma_start(out=xsv[:, 4 * j:4 * j + 4, :], in_=xTv[:, 4 * j:4 * j + 4, tb * 512:(tb + 1) * 512]),
                  writes=[xs.reg(j * 2048, (j + 1) * 2048)], sem="xs%d" % (tb % 2))
        pb = tb % 2
        for k in range(16):
            sq = SQA[k % 4]
            P.add("act", lambda e, sq=sq, xsv=xsv, k=k: e.activation(out=sq.ap, in_=xsv[:, k, :], func=AF.Square),
                  reads=[xs.reg(k * 512, (k + 1) * 512)], writes=[sq.reg()])
            P.add("pe", lambda e, sq=sq, k=k, pb=pb: e.matmul(PS[pb][:, :], lhsT=ones2048, rhs=sq.ap, start=(k == 0), stop=(k == 15)),
                  reads=[sq.reg(), RCB], writes=[psr(pb)])
        r1 = R1[tb % 2]
        rstd_ops(r1, pb)
        for k in range(16):
            P.add("dve", lambda e, k=k, tb=tb, xsv=xsv, r1=r1: e.scalar_tensor_tensor(
                out=XNv[:, k, tb * 512:(tb + 1) * 512], in0=xsv[:, k, :], scalar=g1(k), in1=r1.ap, op0=ALU.mult, op1=ALU.mult),
                reads=[xs.reg(k * 512, (k + 1) * 512), r1.reg(), RCF], writes=[xn_reg(k, tb * 512, (tb + 1) * 512)])
    emit_precast(8)

    o = TB0
    WQ = [Buf(o + i * 12288, 4096, BF16) for i in range(2)]
    WK = [Buf(o + i * 12288 + 4096, 4096, BF16) for i in range(2)]
    WV = [Buf(o + i * 12288 + 8192, 4096, BF16) for i in range(2)]
    o += 24576
    QT = [Buf(o + i * 12288, 4096, BF16) for i in range(2)]
    KT = [Buf(o + i * 12288 + 4096, 4096, BF16) for i in range(2)]
    VV = [Buf(o + i * 12288 + 8192, 4096, BF16) for i in range(2)]
    o += 24576
    SQ = [Buf(o + i * 1024, 1024, BF16) for i in range(2)]
    Q1 = [Buf(o + 2048 + i * 1024, 1024, BF16) for i in range(2)]
    o += 4096
    RS = [Buf(o + i * 2048, 2048, F32) for i in range(2)]
    o += 4096
    T1 = [Buf(o + i * 2048, 2048, F32) for i in range(2)]
    T2 = [Buf(o + 4096 + i * 2048, 2048, F32) for i in range(2)]
    o += 8192
    PT = [Buf(o + i * 1024, 1024, BF16) for i in range(2)]
    o += 2048
    AO = o
    ACCN = Buf(AO, 8192, F32)
    ACCD = Buf(AO + 8192, 8192, F32)
    ROPE = Buf(AO + 16384, 16384, F32)
    ROPEv = ROPE.ap.rearrange("p (c t) -> p c t", t=S)
    RC = [Buf(AO + 32768 + i * 512, 512, F32) for i in range(2)]
    DEN = [Buf(AO + 32768 + 1024 + i * 512, 512, F32) for i in range(2)]
    NUM = [Buf(AO + 32768 + 2048 + i * 512, 512, F32) for i in range(2)]
    MASKC = Buf(AO, 7168, F32)
    RG = Buf(AO + 7168, 7168, F32)
    BT = [Buf(AO + 14336 + i * 7168, 7168, F32) for i in range(2)]
    TMPB = [Buf(AO + 28672 + i * 2048, 2048, F32) for i in range(2)]
    RCB2 = [Buf(AO + 32768 + i * 512, 512, F32) for i in range(2)]

    jobs = []
    for j in range(4):
        for g, dil in enumerate((1, 4, 16)):
            jobs.append(dict(h=4 * g + j, kind="A", dil=dil, j=j, g=g))
    for j in range(4):
        jobs.append(dict(h=12 + j, kind="B", dil=1, j=j, g=0))

    def tok_slice(dil, i):
        M = S // dil
        nt = M // 128
        s_, m0 = i // nt, (i % nt) * 128
        st = s_ + dil * m0
        return slice(st, st + dil * 127 + 1, dil)

    def blk_view(ap2d, dil, jb):
        if dil == 1:
            return ap2d[:, jb * 512:(jb + 1) * 512]
        if dil == 4:
            return ap2d[:, jb:jb + 4 * 511 + 1:4]
        return ap2d.rearrange("p (m r) -> p r m", r=16)[:, 4 * jb:4 * jb + 4, :]

    def shp(ap, dil):
        if dil == 16:
            return ap.rearrange("p (a b) -> p a b", b=128)
        return ap

    def load_head_weights(n):
        jb = jobs[n]
        h, st = jb["h"], n % 2
        for (buf, c0) in ((WQ[st], h * 128), (WK[st], 2048 + h * 128), (WV[st], 4096 + h * 128)):
            P.add("pool", lambda e, buf=buf, c0=c0: e.dma_start(out=buf.ap.rearrange("p (k c) -> p k c", c=128), in_=w_in_v[:, :, c0:c0 + 128]),
                  writes=[buf.reg()], sem="wset%d" % st)

    def setup_tables(n):
        jb = jobs[n]
        bt = BT[n % 2]
        P.add("sp", lambda e, jb=jb: e.dma_start(out=RG.ap, in_=rpbg[jb["j"]]), writes=[RG.reg()], sem="rg")
        P.add("dve", lambda e, bt=bt: e.tensor_tensor(out=bt.ap, in0=RG.ap, in1=MASKC.ap, op=ALU.add),
              reads=[RG.reg(), MASKC.reg()], writes=[bt.reg()])

    inproj_bank = [0]

    def inproj(n):
        jb = jobs[n]
        st, dil, kind = n % 2, jb["dil"], jb["kind"]
        stages = []
        for which in ("q", "k"):
            W = WQ[st] if which == "q" else WK[st]
            Wv_ = W.ap.rearrange("p (k c) -> p k c", c=128)
            dst = QT[st] if which == "q" else KT[st]
            if kind == "A":
                gvec = gqa if which == "q" else gka
            else:
                gvec = gqb if which == "q" else gkb
            for jbk in range(4):
                def main(which=which, Wv_=Wv_, W=W, jbk=jbk, dil=dil, kind=kind, gvec=gvec):
                    b = inproj_bank[0] % 2
                    inproj_bank[0] += 1
                    for k in range(16):
                        P.add("pe", lambda e, k=k, b=b: e.matmul(shp(PS[b][:, :], dil), lhsT=Wv_[:, k, :], rhs=blk_view(XNv[:, k, :], dil, jbk),
                                                                  start=(k == 0), stop=(k == 15)),
                              reads=[W.reg(k * 128, (k + 1) * 128), xn_reg(k)], writes=[psr(b)])
                    sq = SQ[b]
                    P.add("act", lambda e, sq=sq, b=b: e.activation(out=sq.ap, in_=PS[b][:, :], func=AF.Square),
                          reads=[psr(b)], writes=[sq.reg()])
                    if kind == "A":
                        q1 = Q1[b]
                        P.add("act", lambda e, q1=q1, b=b: e.activation(out=q1.ap, in_=PS[b][:, :], func=AF.Copy, scale=gvec),
                              reads=[psr(b), RCF], writes=[q1.reg()])
                    return b

                def post(b, which=which, dst=dst, jbk=jbk, dil=dil, kind=kind, gvec=gvec):
                    sq, rs = SQ[b], RS[b]
                    P.add("pe", lambda e, sq=sq: e.matmul(PS[2][:, :], lhsT=ones128, rhs=sq.ap, start=True, stop=True),
                          reads=[sq.reg(), RCB], writes=[psr(2)])
                    dreg = dst.reg(jbk * 512, (jbk + 1) * 512)
                    dap = dst.ap[:, jbk * 512:(jbk + 1) * 512]
                    if kind == "A":
                        q1, t1, t2 = Q1[b], T1[b], T2[b]
                        P.add("pe", lambda e, q1=q1: e.matmul(PS[3][:, :], lhsT=permT, rhs=q1.ap, start=True, stop=True),
                              reads=[q1.reg(), RCB], writes=[psr(3)])
                        P.add("dve", lambda e, t1=t1, b=b: e.scalar_tensor_tensor(
                            out=shp(t1.ap, dil), in0=shp(PS[b][:, :], dil), scalar=gvec, in1=blk_view(ROPEv[:, 0, :], dil, jbk), op0=ALU.mult, op1=ALU.mult),
                            reads=[psr(b), ROPE.reg(0, S), RCF], writes=[t1.reg()])
                        rstd_ops(rs, 2)
                        P.add("dve", lambda e, t2=t2: e.tensor_tensor(out=shp(t2.ap, dil), in0=shp(PS[3][:, :], dil), in1=blk_view(ROPEv[:, 1, :], dil, jbk), op=ALU.mult),
                              reads=[psr(3), ROPE.reg(S, 2 * S)], writes=[t2.reg()])
                        P.add("dve", lambda e, t1=t1, t2=t2: e.tensor_tensor(out=t1.ap, in0=t1.ap, in1=t2.ap, op=ALU.add),
                              reads=[t1.reg(), t2.reg()], writes=[t1.reg()])
                        P.add("dve", lambda e, t1=t1, rs=rs: e.tensor_tensor(out=dap, in0=t1.ap, in1=rs.ap, op=ALU.mult),
                              reads=[t1.reg(), rs.reg()], writes=[dreg])
                    else:
                        rstd_ops(rs, 2)
                        P.add("dve", lambda e, rs=rs, b=b: e.scalar_tensor_tensor(out=dap, in0=PS[b][:, :], scalar=gvec, in1=rs.ap, op0=ALU.mult, op1=ALU.mult),
                              reads=[psr(b), rs.reg(), RCF], writes=[dreg])
                stages.append((main, post))
        Wvv = WV[st].ap.rearrange("p (k c) -> p k c", c=128)
        Vb = VV[st]
        for grp in range(4):
            def mainv(grp=grp, Wvv=Wvv, Vb=Vb, dil=dil, st=st):
                b = inproj_bank[0] % 2
                inproj_bank[0] += 1
                for ti in range(4):
                    i = grp * 4 + ti
                    sl = tok_slice(dil, i)
                    for k in range(16):
                        P.add("pe", lambda e, k=k, b=b, ti=ti, sl=sl: e.matmul(PS[b][:, ti * 128:(ti + 1) * 128], lhsT=XNv[:, k, sl], rhs=Wvv[:, k, :],
                                                                          start=(k == 0), stop=(k == 15), skip_group_check=True),
                              reads=[WV[st].reg(k * 128, (k + 1) * 128), xn_reg(k)], writes=[psr(b)])
                P.add("act", lambda e, b=b, grp=grp: e.activation(out=Vb.ap[:, grp * 512:(grp + 1) * 512], in_=PS[b][:, :], func=AF.Copy),
                      reads=[psr(b)], writes=[Vb.reg(grp * 512, (grp + 1) * 512)])
                return b
            stages.append((mainv, lambda b: None))
        return stages

    def run_skewed(stages):
        prev = None
        for (m, p_) in stages:
            b = m()
            if prev is not None:
                prev[0](prev[1])
            prev = (p_, b)
        if prev is not None:
            prev[0](prev[1])

    od_count = [0]

    def attention(n):
        jb = jobs[n]
        st, dil, kind, j, g = n % 2, jb["dil"], jb["kind"], jb["j"], jb["g"]
        qt, kt, vb = QT[st], KT[st], VV[st]
        M = S // dil
        nt = M // 128
        stages = []
        ucount = [0]
        qblocks = []
        if kind == "A":
            for T in range(16):
                qi = T % nt
                base = T - qi
                kts = [(s_, base + qi - 1 + s_) for s_ in range(3) if 0 <= qi - 1 + s_ < nt]
                qblocks.append(dict(T=T, units=[kts]))
        else:
            for pb in range(16):
                if pb < 2:
                    tiles, off = [0, 1, 2, 3], 2 * pb
                elif pb >= 14:
                    tiles, off = [12, 13, 14, 15], 4 + 2 * (pb - 14)
                else:
                    tiles, off = [pb - 2 + c for c in range(5)], 4
                lo = tiles[0]
                v = 0 if 2 <= pb < 14 else 1
                u1 = [(tl - lo) for tl in tiles[:4]][::-1]
                units = [[(s_, lo + c) for s_, c in enumerate(u1)]]
                dd0s = [6 - 2 * u1[0] + off]
                if len(tiles) == 5:
                    units.append([(0, lo + 4)])
                    dd0s.append(6 - 2 * 4 + off)
                qblocks.append(dict(T=pb, units=units, v=v, dd0s=dd0s))
        for qb in qblocks:
            T = qb["T"]
            odi = od_count[0]
            od_count[0] += 1
            obank = 6 + (odi // 2) % 2
            ocol = (odi % 2) * 256
            first_in_bank = (odi % 2 == 0)
            nun = len(qb["units"])
            for ui, unit in enumerate(qb["units"]):
                def s_stage(unit=unit, T=T, qb=qb, ui=ui):
                    u = ucount[0]
                    ucount[0] += 1
                    sb_, pt = 4 + u % 2, PT[u % 2]
                    s0 = unit[0][0]
                    ns = len(unit)
                    for (s_, kti) in unit:
                        P.add("pe", lambda e, s_=s_, kti=kti, sb_=sb_: e.matmul(PS[sb_][:, s_ * 128:(s_ + 1) * 128], lhsT=kt.ap[:, kti * 128:(kti + 1) * 128],
                                                                            rhs=qt.ap[:, T * 128:(T + 1) * 128], start=True, stop=True, skip_group_check=True),
                              reads=[kt.reg(kti * 128, (kti + 1) * 128), qt.reg(T * 128, (T + 1) * 128)], writes=[psr(sb_)])
                    c0, c1 = s0 * 128, (s0 + ns) * 128
                    if kind == "A":
                        P.add("act", lambda e, sb_=sb_, pt=pt: e.activation(out=pt.ap[:, c0:c1], in_=PS[sb_][:, c0:c1], func=AF.Exp, scale=SCALE),
                              reads=[psr(sb_)], writes=[pt.reg(c0, c1)])
                        P.add("dve", lambda e, pt=pt: e.tensor_tensor(out=pt.ap[:, c0:c1], in0=pt.ap[:, c0:c1], in1=maskA[:, c0:c1], op=ALU.mult),
                              reads=[pt.reg(c0, c1), RCB], writes=[pt.reg(c0, c1)])
                    else:
                        bt, tmp = BT[n % 2], TMPB[u % 2]
                        v, dd0 = qb["v"], qb["dd0s"][ui]
                        b0 = v * 896 + dd0 * 64
                        P.add("dve", lambda e, sb_=sb_, tmp=tmp, bt=bt, b0=b0: e.scalar_tensor_tensor(
                            out=tmp.ap[:, c0:c1], in0=PS[sb_][:, c0:c1], scalar=SCALE, in1=bt.ap[:, b0:b0 + (c1 - c0)], op0=ALU.mult, op1=ALU.add),
                            reads=[psr(sb_), bt.reg()], writes=[tmp.reg(c0, c1)])
                        P.add("act", lambda e, tmp=tmp, pt=pt: e.activation(out=pt.ap[:, c0:c1], in_=tmp.ap[:, c0:c1], func=AF.Exp),
                              reads=[tmp.reg(c0, c1)], writes=[pt.reg(c0, c1)])
                    return (u, unit)

                def pv_stage(su, ui=ui, nun=nun, obank=obank, ocol=ocol, first_in_bank=first_in_bank, T=T, odi=odi):
                    u, unit = su
                    pt = PT[u % 2]
                    nk = len(unit)
                    for ii, (s_, kti) in enumerate(unit):
                        P.add("pe", lambda e, s_=s_, kti=kti, ii=ii: e.matmul(PS[obank][:, ocol:ocol + 128], lhsT=vb.ap[:, kti * 128:(kti + 1) * 128],
                                                                        rhs=pt.ap[:, s_ * 128:(s_ + 1) * 128],
                                                                        start=(first_in_bank and ui == 0 and ii == 0), stop=(ui == nun - 1 and ii == nk - 1), skip_group_check=True),
                              reads=[vb.reg(kti * 128, (kti + 1) * 128), pt.reg(s_ * 128, (s_ + 1) * 128)], writes=[psr(obank)])
                    for ii, (s_, kti) in enumerate(unit):
                        P.add("pe", lambda e, s_=s_, ii=ii: e.matmul(PS[obank][:, ocol + 128:ocol + 256], lhsT=ones1, rhs=pt.ap[:, s_ * 128:(s_ + 1) * 128],
                                                              start=False, stop=(ui == nun - 1 and ii == nk - 1), skip_group_check=True),
                              reads=[pt.reg(s_ * 128, (s_ + 1) * 128), RCB], writes=[psr(obank)])
                    if ui == nun - 1 and odi % 2 == 1:
                        for (TT, oc) in ((T - 1, 0), (T, 256)):
                            evac(TT, obank, oc)

                stages.append((s_stage, pv_stage))

        def evac(TT, obank, oc):
            Ops = PS[obank][:, oc:oc + 128]
            Dps = PS[obank][:, oc + 128:oc + 256]
            if kind == "B":
                rc = RCB2[TT % 2]
                P.add("dve", lambda e, rc=rc: e.reciprocal(out=rc.ap, in_=Dps), reads=[psr(obank)], writes=[rc.reg()])
                P.add("dve", lambda e, rc=rc: e.tensor_tensor(out=OABv[:, 4 + j, TT * 128:(TT + 1) * 128], in0=Ops, in1=rc.ap, op=ALU.mult),
                      reads=[psr(obank), rc.reg()], writes=[OAB.reg((4 + j) * S + TT * 128, (4 + j) * S + (TT + 1) * 128)])
                return
            sl = tok_slice(dil, TT)
            lo, hi = sl.start, sl.stop
            djk = ("acc%d" % dil, TT // nt) if dil > 1 else None
            accn = ACCN.ap[:, sl]
            accd = ACCD.ap[:, sl]
            rn, rd = ACCN.reg(lo, hi, djk), ACCD.reg(lo, hi, djk)
            if g == 0:
                P.add("act", lambda e: e.activation(out=accn, in_=Ops, func=AF.Copy), reads=[psr(obank)], writes=[rn])
                P.add("dve", lambda e: e.tensor_copy(out=accd, in_=Dps), reads=[psr(obank)], writes=[rd])
            elif g == 1:
                P.add("dve", lambda e: e.tensor_tensor(out=accn, in0=Ops, in1=accn, op=ALU.add), reads=[psr(obank), rn], writes=[rn])
                P.add("dve", lambda e: e.tensor_tensor(out=accd, in0=Dps, in1=accd, op=ALU.add), reads=[psr(obank), rd], writes=[rd])
            else:
                den, rc, num = DEN[TT % 2], RC[TT % 2], NUM[TT % 2]
                P.add("dve", lambda e: e.tensor_tensor(out=den.ap, in0=Dps, in1=accd, op=ALU.add), reads=[psr(obank), rd], writes=[den.reg()])
                P.add("dve", lambda e: e.tensor_tensor(out=num.ap, in0=Ops, in1=accn, op=ALU.add), reads=[psr(obank), rn], writes=[num.reg()])
                P.add("dve", lambda e: e.reciprocal(out=rc.ap, in_=den.ap), reads=[den.reg()], writes=[rc.reg()])
                P.add("dve", lambda e: e.tensor_tensor(out=OABv[:, j, sl], in0=num.ap, in1=rc.ap, op=ALU.mult),
                      reads=[num.reg(), rc.reg()], writes=[OAB.reg(j * S + lo, j * S + hi, djk)])
        return stages

    P.add("sp", lambda e: e.dma_start(out=ROPEv[:, :, :], in_=roped[:, :, :]), writes=[ROPE.reg()], sem="rope")
    load_head_weights(0)
    load_head_weights(1)
    run_skewed(inproj(0))
    for n in range(16):
        if n + 1 < 16:
            run_skewed(inproj(n + 1))
        run_skewed(attention(n))
        if n == 11:
            P.add("sp", lambda e: e.dma_start(out=MASKC.ap, in_=maskcd[:, :]), writes=[MASKC.reg()], sem="maskc")
        if n + 1 < 16 and jobs[n + 1]["kind"] == "B":
            setup_tables(n + 1)
        if n + 2 < 16:
            load_head_weights(n + 2)
        emit_precast(5)

    if debug:
        P.add("sp", lambda e: e.dma_start(out=dbg_xn[:, :], in_=XN.ap), reads=[XN.reg()], sem="dbg")
        P.add("sp", lambda e: e.dma_start(out=dbg_oab[:, :], in_=OAB.ap), reads=[OAB.reg()], sem="dbg")

    MIX = Buf(TB0, 65536, BF16)
    MIXv = MIX.ap.rearrange("p (tb k t) -> p tb k t", k=16, t=512)
    o = TB0 + 65536
    WGA = [Buf(o + i * 8192, 4096, BF16) for i in range(2)]
    WGB = [Buf(o + i * 8192 + 4096, 4096, BF16) for i in range(2)]
    o += 16384
    PA = [Buf(o + i * 2048, 1024, BF16) for i in range(2)]
    PBb = [Buf(o + i * 2048 + 1024, 1024, BF16) for i in range(2)]
    o += 4096
    SA = [Buf(o + i * 8192, 2048, F32) for i in range(2)]
    SBg = [Buf(o + i * 8192 + 2048, 2048, F32) for i in range(2)]
    M1 = [Buf(o + i * 8192 + 4096, 2048, F32) for i in range(2)]
    M2 = [Buf(o + i * 8192 + 6144, 2048, F32) for i in range(2)]
    w_pa_v = w_pa.rearrange("(s p) c -> p s c", p=128)
    w_pb_v = w_pb.rearrange("(s p) c -> p s c", p=128)

    def load_c1(c):
        st = c % 2
        P.add("pool", lambda e: e.dma_start(out=WGA[st].ap.rearrange("p (k c) -> p k c", c=128), in_=w_in_v[:, :, 6144 + c * 128:6144 + (c + 1) * 128]),
              writes=[WGA[st].reg()], sem="wg%d" % st)
        P.add("pool", lambda e: e.dma_start(out=WGB[st].ap.rearrange("p (k c) -> p k c", c=128), in_=w_in_v[:, :, 8192 + c * 128:8192 + (c + 1) * 128]),
              writes=[WGB[st].reg()], sem="wg%d" % st)
        P.add("pool", lambda e: e.dma_start(out=PA[st].ap.rearrange("p (s c) -> p s c", c=128), in_=w_pa_v[:, :, c * 128:(c + 1) * 128]),
              writes=[PA[st].reg()], sem="wg%d" % st)
        P.add("pool", lambda e: e.dma_start(out=PBb[st].ap.rearrange("p (s c) -> p s c", c=128), in_=w_pb_v[:, :, c * 128:(c + 1) * 128]),
              writes=[PBb[st].reg()], sem="wg%d" % st)

    load_c1(0)
    load_c1(1)
    cnt = 0
    for c in range(16):
        st = c % 2
        wga = WGA[st].ap.rearrange("p (k c) -> p k c", c=128)
        wgb = WGB[st].ap.rearrange("p (k c) -> p k c", c=128)
        pa = PA[st].ap.rearrange("p (s c) -> p s c", c=128)
        pbm = PBb[st].ap.rearrange("p (s c) -> p s c", c=128)
        for tb in range(4):
            bs = 4 * (cnt % 2)
            ss = cnt % 2
            cnt += 1
            tsl = slice(tb * 512, (tb + 1) * 512)
            for k in range(16):
                P.add("pe", lambda e, k=k, bs=bs, wga=wga, tsl=tsl: e.matmul(PS[bs][:, :], lhsT=wga[:, k, :], rhs=XNv[:, k, tsl], start=(k == 0), stop=(k == 15)),
                      reads=[WGA[st].reg(), xn_reg(k, tb * 512, (tb + 1) * 512)], writes=[psr(bs)])
            for k in range(16):
                P.add("pe", lambda e, k=k, bs=bs, wgb=wgb, tsl=tsl: e.matmul(PS[bs + 1][:, :], lhsT=wgb[:, k, :], rhs=XNv[:, k, tsl], start=(k == 0), stop=(k == 15)),
                      reads=[WGB[st].reg(), xn_reg(k, tb * 512, (tb + 1) * 512)], writes=[psr(bs + 1)])
            for s_ in range(4):
                P.add("pe", lambda e, s_=s_, bs=bs, pa=pa, tsl=tsl: e.matmul(PS[bs + 2][:, :], lhsT=pa[:, s_, :], rhs=OABv[:, s_, tsl], start=(s_ == 0), stop=(s_ == 3)),
                      reads=[PA[st].reg(), OAB.reg(s_ * S + tb * 512, s_ * S + (tb + 1) * 512)], writes=[psr(bs + 2)])
            for s_ in range(4):
                P.add("pe", lambda e, s_=s_, bs=bs, pbm=pbm, tsl=tsl: e.matmul(PS[bs + 3][:, :], lhsT=pbm[:, s_, :], rhs=OABv[:, 4 + s_, tsl], start=(s_ == 0), stop=(s_ == 3)),
                      reads=[PBb[st].reg(), OAB.reg((4 + s_) * S + tb * 512, (4 + s_) * S + (tb + 1) * 512)], writes=[psr(bs + 3)])
            sa, sb2, m1, m2 = SA[ss], SBg[ss], M1[ss], M2[ss]
            P.add("act", lambda e, sa=sa, bs=bs, c=c: e.activation(out=sa.ap, in_=PS[bs][:, :], func=AF.Sigmoid, bias=bgc(c), scale=1.0),
                  reads=[psr(bs), RCF], writes=[sa.reg()])
            P.add("act", lambda e, sb2=sb2, bs=bs, c=c: e.activation(out=sb2.ap, in_=PS[bs + 1][:, :], func=AF.Sigmoid, bias=bgc(16 + c), scale=1.0),
                  reads=[psr(bs + 1), RCF], writes=[sb2.reg()])
            P.add("dve", lambda e, m1=m1, sa=sa, bs=bs: e.tensor_tensor(out=m1.ap, in0=PS[bs + 2][:, :], in1=sa.ap, op=ALU.mult),
                  reads=[psr(bs + 2), sa.reg()], writes=[m1.reg()])
            P.add("dve", lambda e, m2=m2, sb2=sb2, bs=bs: e.tensor_tensor(out=m2.ap, in0=PS[bs + 3][:, :], in1=sb2.ap, op=ALU.mult),
                  reads=[psr(bs + 3), sb2.reg()], writes=[m2.reg()])
            mo = (tb * 16 + c) * 512
            P.add("dve", lambda e, m1=m1, m2=m2, tb=tb, c=c: e.tensor_tensor(out=MIXv[:, tb, c, :], in0=m1.ap, in1=m2.ap, op=ALU.add),
                  reads=[m1.reg(), m2.reg()], writes=[MIX.reg(mo, mo + 512)])
        if c + 2 < 16:
            load_c1(c + 2)
        emit_precast(4)
    emit_precast(1000)
    if debug:
        P.add("sp", lambda e: e.dma_start(out=dbg_mix[:, :], in_=MIX.ap), reads=[MIX.reg()], sem="dbg")

    U2 = Buf(0, 65536, BF16)
    U2v = U2.ap.rearrange("p (f t) -> p f t", t=512)
    Hb = Buf(65536, 32768, F32)
    Hv = Hb.ap.rearrange("p (c t) -> p c t", t=512)
    o = TB0 + 65536
    HN = Buf(o, 16384, BF16)
    HNv = HN.ap.rearrange("p (k t) -> p k t", t=512)
    o += 16384
    NRING = 4
    RING = [Buf(o + i * 4096, 4096, BF16) for i in range(NRING)]
    o += NRING * 4096
    TT_ = [Buf(o + i * 2048, 2048, F32) for i in range(2)]
    o += 4096
    SQD = [Buf(o + i * 1024, 1024, BF16) for i in range(2)]
    o += 2048
    R2 = Buf(o, 2048, F32)

    stream = []
    for tb in range(4):
        for c in range(16):
            stream.append((s_out[c], Reg("s_out", c, c + 1)))
        for f in range(64):
            stream.append((s_up[f], Reg("s_up", f, f + 1)))
        for cp in range(8):
            for fg in range(4):
                for c in (2 * cp, 2 * cp + 1):
                    t = c * 4 + fg
                    stream.append((s_dn[t], Reg("s_dn", t, t + 1)))
    spos = [0]
    PREF = 3

    def ring_load_upto(i):
        while spos[0] <= min(i, len(stream) - 1):
            n_ = spos[0]
            src, rg = stream[n_]
            rb = RING[n_ % NRING]
            P.add("sp", lambda e, src=src, rb=rb: e.dma_start(out=rb.ap, in_=src), reads=[rg], writes=[rb.reg()], sem="ring%d" % (n_ % NRING))
            spos[0] += 1

    tix = [0]

    def next_tile():
        n_ = tix[0]
        tix[0] += 1
        ring_load_upto(n_ + PREF)
        rb = RING[n_ % NRING]
        return rb, rb.ap.rearrange("p (k c) -> p k c", c=128)

    def xload(tb, c):
        P.add("sp", lambda e: e.dma_start(out=Hv[:, c, :], in_=xTv[:, c, tb * 512:(tb + 1) * 512]),
              writes=[Hb.reg(c * 512, (c + 1) * 512)], sem="hx")

    upbank = [0]
    outs = []
    for tb in range(4):
        xload(tb, 0)
        xload(tb, 1)
        pend = None
        for c in range(16):
            if c + 2 < 16:
                xload(tb, c + 2)
            rb, rv = next_tile()
            b = upbank[0] % 2
            upbank[0] += 1
            for k in range(16):
                P.add("pe", lambda e, k=k, b=b, rv=rv, tb=tb: e.matmul(PS[b][:, :], lhsT=rv[:, k, :], rhs=MIXv[:, tb, k, :], start=(k == 0), stop=(k == 15)),
                      reads=[rb.reg(), MIX.reg((tb * 16 + k) * 512, (tb * 16 + k + 1) * 512)], writes=[psr(b)])
            hreg = Hb.reg(c * 512, (c + 1) * 512)
            P.add("dve", lambda e, b=b, c=c: e.tensor_tensor(out=Hv[:, c, :], in0=PS[b][:, :], in1=Hv[:, c, :], op=ALU.add),
                  reads=[psr(b), hreg], writes=[hreg])
            P.add("act", lambda e, c=c: e.activation(out=HNv[:, c, :], in_=Hv[:, c, :], func=AF.Copy, scale=g2(c)),
                  reads=[hreg, RCF], writes=[HN.reg(c * 512, (c + 1) * 512)])
            sq = SQD[c % 2]
            P.add("act", lambda e, c=c, sq=sq: e.activation(out=sq.ap, in_=Hv[:, c, :], func=AF.Square),
                  reads=[hreg], writes=[sq.reg()])
            if pend is not None:
                pend()
            pend = (lambda sq=sq, c=c: P.add("pe", lambda e: e.matmul(PS[2][:, :], lhsT=ones2048, rhs=sq.ap, start=(c == 0), stop=(c == 15)),
                                             reads=[sq.reg(), RCB], writes=[psr(2)]))
        pend()
        rstd_ops(R2, 2)
        for f in range(64):
            rb, rv = next_tile()
            b = upbank[0] % 2
            upbank[0] += 1
            for k in range(16):
                P.add("pe", lambda e, k=k, b=b, rv=rv: e.matmul(PS[b][:, :], lhsT=rv[:, k, :], rhs=HNv[:, k, :], start=(k == 0), stop=(k == 15)),
                      reads=[rb.reg(), HN.reg(k * 512, (k + 1) * 512)], writes=[psr(b)])
            tt = TT_[f % 2]
            P.add("dve", lambda e, b=b, tt=tt: e.scalar_tensor_tensor(out=tt.ap, in0=PS[b][:, :], scalar=0.0, in1=R2.ap, op0=ALU.max, op1=ALU.mult),
                  reads=[psr(b), R2.reg()], writes=[tt.reg()])
            P.add("act", lambda e, f=f, tt=tt: e.activation(out=U2v[:, f, :], in_=tt.ap, func=AF.Square),
                  reads=[tt.reg()], writes=[U2.reg(f * 512, (f + 1) * 512)])
        for cp in range(8):
            banks = (3, 4) if cp % 2 == 0 else (5, 6)
            for fg in range(4):
                for ci, c in enumerate((2 * cp, 2 * cp + 1)):
                    rb, rv = next_tile()
                    bk = banks[ci]
                    for kk in range(16):
                        f = fg * 16 + kk
                        P.add("pe", lambda e, kk=kk, f=f, bk=bk, rv=rv, fg=fg: e.matmul(PS[bk][:, :], lhsT=rv[:, kk, :], rhs=U2v[:, f, :],
                                                                              start=(fg == 0 and kk == 0), stop=(fg == 3 and kk == 15)),
                              reads=[rb.reg(), U2.reg(f * 512, (f + 1) * 512)], writes=[psr(bk)])
            for ci, c in enumerate((2 * cp, 2 * cp + 1)):
                bk = banks[ci]
                hreg = Hb.reg(c * 512, (c + 1) * 512)
                P.add("dve", lambda e, bk=bk, c=c: e.tensor_tensor(out=Hv[:, c, :], in0=PS[bk][:, :], in1=Hv[:, c, :], op=ALU.add),
                      reads=[psr(bk), hreg], writes=[hreg])
                outs.append(P.add("sp", lambda e, c=c, tb=tb: e.dma_start(out=outTv[:, c, tb * 512:(tb + 1) * 512], in_=Hv[:, c, :]),
                                  reads=[hreg], sem="ost"))
    final_sems = ["ost"] + (["dbg"] if debug else [])

    from contextlib import ExitStack
    with ExitStack() as es:
        es.enter_context(nc.allow_low_precision("bf16 matmul operands, fp32 accumulation"))
        esems = {e: es.enter_context(nc.semaphore("e_" + e)) for e in ("pe", "act", "dve")}
        dsems = {name: es.enter_context(nc.semaphore("d_" + name)) for name in P.dsem}
        block = es.enter_context(nc.Block())
        run = P.emit(esems, dsems)

        def mk(ename):
            def f(eng):
                run(ename, eng)
                if ename == "sp":
                    for sname in final_sems:
                        eng.wait_ge(dsems[sname], P.dsem[sname])
            return f
        block.tensor(mk("pe"))
        block.scalar(mk("act"))
        block.vector(mk("dve"))
        block.gpsimd(mk("pool"))
        block.sync(mk("sp"))
    return nc


_CACHE = {}


def _prep_inputs(inp):
    cb, rope, maskc = _const_tables()
    f32 = np.float32
    cvec = np.zeros((128, 69), f32)
    cvec[:, 68] = EPS
    cvec[:, 0:16] = np.asarray(inp["norm_mix"], f32).reshape(16, 128).T
    cvec[:, 16:32] = np.asarray(inp["norm_ffn"], f32).reshape(16, 128).T
    cvec[:, 32:64] = np.asarray(inp["b_gate"], f32).reshape(32, 128).T
    cvec[:, 64] = np.asarray(inp["q_norm_a"], f32).reshape(128)
    cvec[:, 65] = np.asarray(inp["k_norm_a"], f32).reshape(128)
    cvec[:, 66] = np.asarray(inp["q_norm_b"], f32).reshape(128)
    cvec[:, 67] = np.asarray(inp["k_norm_b"], f32).reshape(128)
    shared = {
        "w_in": np.ascontiguousarray(np.asarray(inp["w_in"], f32)[0]),
        "w_pa": np.ascontiguousarray(np.asarray(inp["w_proj_a"], f32)[0]),
        "w_pb": np.ascontiguousarray(np.asarray(inp["w_proj_b"], f32)[0]),
        "w_out": np.ascontiguousarray(np.asarray(inp["w_out"], f32)[0]),
        "w_up": np.ascontiguousarray(np.asarray(inp["w_up"], f32)[0]),
        "w_dn": np.ascontiguousarray(np.asarray(inp["w_down"], f32)[0]),
        "cvec": cvec, "cb": cb, "rope": rope, "maskc": maskc,
        "rpbg": _rpb_gather(np.asarray(inp["rpb_b"], f32)[0]),
    }
    x = np.asarray(inp["x"], f32)
    in_maps = []
    for b in range(NCORES):
        m = dict(shared)
        m["xT"] = np.ascontiguousarray(x[b].T)
        in_maps.append(m)
    return in_maps


def kernel(**inputs):
    if "nc" not in _CACHE:
        _CACHE["nc"] = build(debug=False)
    nc = _CACHE["nc"]
    in_maps = _prep_inputs(inputs)
    res = run_bass_kernel_spmd(nc, in_maps, core_ids=list(range(NCORES)))
    out = np.stack([np.ascontiguousarray(res.results[b]["outT"].T) for b in range(NCORES)], axis=0)
    return out.astype(np.float32)
```

```python
import numpy as np
import ml_dtypes
import concourse.bass as bass
import concourse.mybir as mybir
from concourse.bass_utils import run_bass_kernel_spmd

F32 = mybir.dt.float32
BF16 = mybir.dt.bfloat16
AF = mybir.ActivationFunctionType
ALU = mybir.AluOpType

S = 2048
D = 2048
NCORES = 8
EPS = 1e-6
SCALE = 128.0 ** -0.5
NEG = -30000.0
ENGS = ("pe", "act", "dve", "pool", "sp")


class Op:
    __slots__ = ("eng", "fn", "idx", "waits", "signaled", "dma_sem", "dma_val")


class Reg:
    __slots__ = ("arena", "lo", "hi", "dj")

    def __init__(self, arena, lo, hi, dj=None):
        self.arena, self.lo, self.hi, self.dj = arena, lo, hi, dj


class Prog:
    def __init__(self, nc):
        self.nc = nc
        self.q = {e: [] for e in ENGS}
        self.known = {e: {} for e in ENGS}
        self.recs = {}
        self.excl = set()
        self.dsem = {}

    def _overl(self, reg):
        out = []
        for rec in self.recs.get(reg.arena, ()):
            if rec[0] < reg.hi and reg.lo < rec[1]:
                d0 = rec[2]
                if d0 is not None and reg.dj is not None and d0[0] == reg.dj[0] and d0[1] != reg.dj[1]:
                    continue
                out.append(rec)
        return out

    def add(self, eng, fn, reads=(), writes=(), sem=None):
        op = Op()
        op.eng, op.fn, op.idx, op.signaled = eng, fn, len(self.q[eng]), False
        op.dma_sem, op.dma_val = sem, 0
        deps = {}

        def need(o):
            if o is None:
                return
            if o.dma_sem is not None:
                key, val = ("s", o.dma_sem), self.dsem[o.dma_sem]
            else:
                if o.eng == "pe" and eng == "pe":
                    return
                key, val = ("e", o.eng), o.idx
            if deps.get(key, -1) < val:
                deps[key] = val

        racc, wacc = [], []
        for r in reads:
            (wacc if r.arena in self.excl else racc).append(r)
        wacc.extend(writes)
        for r in racc:
            for rec in self._overl(r):
                need(rec[3])
        for r in wacc:
            for rec in self._overl(r):
                need(rec[3])
                for o in rec[4].values():
                    need(o)
        waits = []
        kn = self.known[eng]
        for key, val in deps.items():
            if kn.get(key, -1) >= val:
                continue
            kn[key] = val
            waits.append((key, val))
            if key[0] == "e":
                self.q[key[1]][val].signaled = True
        op.waits = waits
        if sem is not None:
            self.dsem[sem] = self.dsem.get(sem, 0) + 16
            op.dma_val = self.dsem[sem]
        for r in racc:
            ov = self._overl(r)
            contained = False
            for rec in ov:
                rec[4][eng if sem is None else ("d", sem, op.idx)] = op
                if rec[0] <= r.lo and r.hi <= rec[1]:
                    contained = True
            if not contained:
                self.recs.setdefault(r.arena, []).append([r.lo, r.hi, r.dj, None, {(eng if sem is None else ("d", sem, op.idx)): op}])
        for r in wacc:
            lst = self.recs.setdefault(r.arena, [])
            ov = self._overl(r)
            for rec in ov:
                if r.lo <= rec[0] and rec[1] <= r.hi:
                    lst.remove(rec)
            lst.append([r.lo, r.hi, r.dj, op, {}])
        self.q[eng].append(op)
        return op

    def emit(self, esems, dsems):
        tick = {}
        for e in ENGS:
            c = 0
            t = []
            for op in self.q[e]:
                if op.signaled:
                    c += 1
                t.append(c)
            tick[e] = t
        self.tick = tick

        def run(e, eng):
            for op in self.q[e]:
                for key, val in op.waits:
                    if key[0] == "e":
                        eng.wait_ge(esems[key[1]], tick[key[1]][val])
                    else:
                        eng.wait_ge(dsems[key[1]], val)
                ins = op.fn(eng)
                if op.dma_sem is not None:
                    ins.then_inc(dsems[op.dma_sem], 16)
                elif op.signaled:
                    ins.then_inc(esems[e], 1)
        return run


def _const_tables():
    bf = ml_dtypes.bfloat16
    cb = np.zeros((128, 7, 128), np.float32)
    cb[:, 0, :] = 1.0 / 2048.0
    cb[:, 1, :] = 1.0 / 128.0
    cb[:, 2, :] = 1.0
    ein = np.arange(128)[:, None]
    eout = np.arange(128)[None, :]
    cb[:, 3, :] = (ein == (eout + 64) % 128)
    kk = np.arange(128)[:, None]
    qq = np.arange(128)[None, :]
    cb[:, 4, :] = (kk - qq >= 64)
    cb[:, 5, :] = (np.abs(kk - qq) <= 64)
    cb[:, 6, :] = (qq - kk >= 64)
    cb = cb.reshape(128, 7 * 128).astype(bf)
    pos = np.arange(S, dtype=np.float32)
    inv = (np.float32(10000.0) ** (-np.arange(0, 128, 2, dtype=np.float32) / np.float32(128))).astype(np.float32)
    ang = (pos[:, None] * inv[None, :]).astype(np.float32)
    cos = np.cos(ang).astype(np.float32).T
    sin = np.sin(ang).astype(np.float32).T
    cosT = np.concatenate([cos, cos], axis=0)
    sinT = np.concatenate([-sin, sin], axis=0)
    rope = np.ascontiguousarray(np.stack([cosT, sinT], axis=1))
    maskc = np.zeros((128, 2, 14, 64), np.float32)
    cq = np.arange(64)
    cstart = np.clip(cq - 8, 0, 48)
    for jj in range(2):
        for ck in range(64):
            p = jj * 64 + ck
            colv = (ck >= cstart) & (ck < cstart + 16)
            for v in range(2):
                for dd in range(14):
                    e = 13 - dd + jj
                    ok = colv & ((v == 1) or (3 <= e <= 10))
                    maskc[p, v, dd, :] = np.where(ok, 0.0, NEG)
    return cb, rope, maskc.reshape(128, 2 * 14 * 64)


def _rpb_gather(rpb):
    jj = np.arange(2)[:, None, None, None]
    ck = np.arange(64)[None, :, None, None]
    dd = np.arange(14)[None, None, :, None]
    cq = np.arange(64)[None, None, None, :]
    e = np.broadcast_to(13 - dd + jj, (2, 64, 14, 64))
    dc = np.broadcast_to(np.clip(ck - cq, -15, 15) + 15, (2, 64, 14, 64))
    g = rpb[:, e, dc]
    g = g.reshape(4, 128, 1, 14, 64)
    g = np.broadcast_to(g, (4, 128, 2, 14, 64))
    return np.ascontiguousarray(g).reshape(4, 128, 2 * 14 * 64).astype(np.float32)


SB_BYTES = 208896


def build(debug=False):
    nc = bass.Bass("TRN2", target_bir_lowering=False)
    P = Prog(nc)

    def dram(name, shape, dt, kind="ExternalInput"):
        return nc.dram_tensor(name, list(shape), dt, kind=kind).ap()

    xT = dram("xT", [D, S], F32)
    w_in = dram("w_in", [D, 10240], F32)
    w_pa = dram("w_pa", [512, D], F32)
    w_pb = dram("w_pb", [512, D], F32)
    w_out = dram("w_out", [D, D], F32)
    w_up = dram("w_up", [D, 8192], F32)
    w_dn = dram("w_dn", [8192, D], F32)
    cvec = dram("cvec", [128, 69], F32)
    cbd = dram("cb", [128, 896], BF16)
    roped = dram("rope", [128, 2, S], F32)
    maskcd = dram("maskc", [128, 1792], F32)
    rpbg = dram("rpbg", [4, 128, 1792], F32)
    outT = dram("outT", [D, S], F32, kind="ExternalOutput")
    s_out = dram("s_out", [16, 128, 2048], BF16, kind="Internal")
    s_up = dram("s_up", [64, 128, 2048], BF16, kind="Internal")
    s_dn = dram("s_dn", [64, 128, 2048], BF16, kind="Internal")
    if debug:
        dbg_xn = dram("dbg_xn", [128, 16 * S], BF16, kind="ExternalOutput")
        dbg_oab = dram("dbg_oab", [128, 8 * S], BF16, kind="ExternalOutput")
        dbg_mix = dram("dbg_mix", [128, 4 * 16 * 512], BF16, kind="ExternalOutput")

    SBt = nc.alloc_sbuf_tensor("SB", [128, SB_BYTES // 2], BF16).ap()
    CF = nc.alloc_sbuf_tensor("CF", [128, 69], F32).ap()
    CB = nc.alloc_sbuf_tensor("CBt", [128, 896], BF16).ap()
    PS = [nc.alloc_psum_tensor("ps%d" % i, [128, 512], F32).ap() for i in range(8)]
    for i in range(8):
        P.excl.add("PS%d" % i)

    def psr(i):
        return Reg("PS%d" % i, 0, 2048)

    class Buf:
        def __init__(self, off, nbytes, dt):
            self.off, self.nbytes, self.dt = off, nbytes, dt
            assert off % 4 == 0 and off + nbytes <= SB_BYTES, (off, nbytes)
            a = SBt[:, off // 2:(off + nbytes) // 2]
            self.ap = a.bitcast(F32) if dt == F32 else a
            self.esz = 4 if dt == F32 else 2

        def reg(self, lo=None, hi=None, dj=None):
            if lo is None:
                return Reg("SB", self.off, self.off + self.nbytes, dj)
            return Reg("SB", self.off + lo * self.esz, self.off + hi * self.esz, dj)

    def creg(name):
        return Reg(name, 0, 1)

    g1 = lambda k: CF[:, k:k + 1]
    g2 = lambda k: CF[:, 16 + k:17 + k]
    bgc = lambda c: CF[:, 32 + c:33 + c]
    gqa, gka, gqb, gkb = CF[:, 64:65], CF[:, 65:66], CF[:, 66:67], CF[:, 67:68]
    ones2048 = CB[:, 0:128]
    ones128 = CB[:, 128:256]
    ones1 = CB[:, 256:384]
    permT = CB[:, 384:512]
    maskA = CB[:, 512:896]

    epsc = CF[:, 68:69]

    def rstd_ops(dst, pbank):
        P.add("act", lambda e: e.activation(out=dst.ap, in_=PS[pbank][:, :], func=AF.Sqrt, bias=epsc, scale=1.0),
              reads=[psr(pbank), RCF], writes=[dst.reg()])
        P.add("dve", lambda e: e.reciprocal(out=dst.ap, in_=dst.ap), reads=[dst.reg()], writes=[dst.reg()])

    P.add("sp", lambda e: e.dma_start(out=CF[:, :], in_=cvec[:, :]), writes=[creg("CF")], sem="const")
    P.add("sp", lambda e: e.dma_start(out=CB[:, :], in_=cbd[:, :]), writes=[creg("CB")], sem="const")
    RCF, RCB = creg("CF"), creg("CB")

    xTv = xT.rearrange("(k p) t -> p k t", p=128)
    outTv = outT.rearrange("(k p) t -> p k t", p=128)
    w_in_v = w_in.rearrange("(k p) c -> p k c", p=128)

    XN = Buf(0, 65536, BF16)
    XNv = XN.ap.rearrange("p (k t) -> p k t", t=S)
    OAB = Buf(65536, 32768, BF16)
    OABv = OAB.ap.rearrange("p (j t) -> p j t", t=S)
    TB0 = 98304

    def xn_reg(k, t0=0, t1=S):
        return XN.reg(k * S + t0, k * S + t1)

    precast_jobs = []
    w_out_v = w_out.rearrange("(k p) c -> p k c", p=128)
    w_up_v = w_up.rearrange("(k p) c -> p k c", p=128)
    w_dn_v = w_dn.rearrange("(g k p) c -> g p k c", p=128, k=16)
    for c in range(16):
        precast_jobs.append((s_out[c].rearrange("p (k c) -> p k c", c=128), w_out_v[:, :, c * 128:(c + 1) * 128], Reg("s_out", c, c + 1)))
    for f in range(64):
        precast_jobs.append((s_up[f].rearrange("p (k c) -> p k c", c=128), w_up_v[:, :, f * 128:(f + 1) * 128], Reg("s_up", f, f + 1)))
    for c in range(16):
        for fg in range(4):
            t = c * 4 + fg
            precast_jobs.append((s_dn[t].rearrange("p (k c) -> p k c", c=128), w_dn_v[fg][:, :, c * 128:(c + 1) * 128], Reg("s_dn", t, t + 1)))
    precast_pos = [0]

    def emit_precast(n):
        for _ in range(n):
            if precast_pos[0] >= len(precast_jobs):
                return
            dst, src, rg = precast_jobs[precast_pos[0]]
            precast_pos[0] += 1
            P.add("pool", lambda e, dst=dst, src=src: e.dma_start(out=dst, in_=src), writes=[rg], sem="precast")

    XS = [Buf(TB0 + 24576, 32768, F32), Buf(TB0 + 24576 + 32768, 32768, F32)]
    SQA = [Buf(TB0 + 24576 + 65536 + i * 1024, 1024, BF16) for i in range(4)]
    R1 = [Buf(TB0 + 24576 + 65536 + 4096 + i * 2048, 2048, F32) for i in range(2)]
    for tb in range(4):
        xs = XS[tb % 2]
        xsv = xs.ap.rearrange("p (k t) -> p k t", t=512)
        for j in range(4):
            P.add("sp", lambda e, xsv=xsv, j=j, tb=tb: e.dma_start(out=xsv[:, 4 * j:4 * j + 4, :], in_=xTv[:, 4 * j:4 * j + 4, tb * 512:(tb + 1) * 512]),
                  writes=[xs.reg(j * 2048, (j + 1) * 2048)], sem="xs%d" % (tb % 2))
        pb = tb % 2
        for k in range(16):
            sq = SQA[k % 4]
            P.add("act", lambda e, sq=sq, xsv=xsv, k=k: e.activation(out=sq.ap, in_=xsv[:, k, :], func=AF.Square),
                  reads=[xs.reg(k * 512, (k + 1) * 512)], writes=[sq.reg()])
            P.add("pe", lambda e, sq=sq, k=k, pb=pb: e.matmul(PS[pb][:, :], lhsT=ones2048, rhs=sq.ap, start=(k == 0), stop=(k == 15)),
                  reads=[sq.reg(), RCB], writes=[psr(pb)])
        r1 = R1[tb % 2]
        rstd_ops(r1, pb)
        for k in range(16):
            P.add("dve", lambda e, k=k, tb=tb, xsv=xsv, r1=r1: e.scalar_tensor_tensor(
                out=XNv[:, k, tb * 512:(tb + 1) * 512], in0=xsv[:, k, :], scalar=g1(k), in1=r1.ap, op0=ALU.mult, op1=ALU.mult),
                reads=[xs.reg(k * 512, (k + 1) * 512), r1.reg(), RCF], writes=[xn_reg(k, tb * 512, (tb + 1) * 512)])

    o = TB0
    WQ = [Buf(o + i * 12288, 4096, BF16) for i in range(2)]
    WK = [Buf(o + i * 12288 + 4096, 4096, BF16) for i in range(2)]
    WV = [Buf(o + i * 12288 + 8192, 4096, BF16) for i in range(2)]
    o += 24576
    QT = [Buf(o + i * 12288, 4096, BF16) for i in range(2)]
    KT = [Buf(o + i * 12288 + 4096, 4096, BF16) for i in range(2)]
    VV = [Buf(o + i * 12288 + 8192, 4096, BF16) for i in range(2)]
    o += 24576
    SQ = [Buf(o + i * 1024, 1024, BF16) for i in range(2)]
    Q1 = [Buf(o + 2048 + i * 1024, 1024, BF16) for i in range(2)]
    o += 4096
    RS = [Buf(o + i * 2048, 2048, F32) for i in range(2)]
    o += 4096
    T1 = [Buf(o + i * 2048, 2048, F32) for i in range(2)]
    T2 = [Buf(o + 4096 + i * 2048, 2048, F32) for i in range(2)]
    o += 8192
    PT = [Buf(o + i * 1024, 1024, BF16) for i in range(2)]
    o += 2048
    AO = o
    ACCN = Buf(AO, 8192, F32)
    ACCD = Buf(AO + 8192, 8192, F32)
    ROPE = Buf(AO + 16384, 16384, F32)
    ROPEv = ROPE.ap.rearrange("p (c t) -> p c t", t=S)
    RC = [Buf(AO + 32768 + i * 512, 512, F32) for i in range(2)]
    DEN = [Buf(AO + 32768 + 1024 + i * 512, 512, F32) for i in range(2)]
    NUM = [Buf(AO + 32768 + 2048 + i * 512, 512, F32) for i in range(2)]
    MASKC = Buf(AO, 7168, F32)
    RG = Buf(AO + 7168, 7168, F32)
    BT = [Buf(AO + 14336 + i * 7168, 7168, F32) for i in range(2)]
    TMPB = [Buf(AO + 28672 + i * 2048, 2048, F32) for i in range(2)]
    RCB2 = [Buf(AO + 32768 + i * 512, 512, F32) for i in range(2)]

    jobs = []
    for j in range(4):
        for g, dil in enumerate((1, 4, 16)):
            jobs.append(dict(h=4 * g + j, kind="A", dil=dil, j=j, g=g))
    for j in range(4):
        jobs.append(dict(h=12 + j, kind="B", dil=1, j=j, g=0))

    def tok_slice(dil, i):
        M = S // dil
        nt = M // 128
        s_, m0 = i // nt, (i % nt) * 128
        st = s_ + dil * m0
        return slice(st, st + dil * 127 + 1, dil)

    def blk_view(ap2d, dil, jb):
        if dil == 1:
            return ap2d[:, jb * 512:(jb + 1) * 512]
        if dil == 4:
            return ap2d[:, jb:jb + 4 * 511 + 1:4]
        return ap2d.rearrange("p (m r) -> p r m", r=16)[:, 4 * jb:4 * jb + 4, :]

    def shp(ap, dil):
        if dil == 16:
            return ap.rearrange("p (a b) -> p a b", b=128)
        return ap

    def load_head_weights(n):
        jb = jobs[n]
        h, st = jb["h"], n % 2
        for (buf, c0) in ((WQ[st], h * 128), (WK[st], 2048 + h * 128), (WV[st], 4096 + h * 128)):
            P.add("pool", lambda e, buf=buf, c0=c0: e.dma_start(out=buf.ap.rearrange("p (k c) -> p k c", c=128), in_=w_in_v[:, :, c0:c0 + 128]),
                  writes=[buf.reg()], sem="wset%d" % st)

    def setup_tables(n):
        jb = jobs[n]
        bt = BT[n % 2]
        P.add("sp", lambda e, jb=jb: e.dma_start(out=RG.ap, in_=rpbg[jb["j"]]), writes=[RG.reg()], sem="rg")
        P.add("dve", lambda e, bt=bt: e.tensor_tensor(out=bt.ap, in0=RG.ap, in1=MASKC.ap, op=ALU.add),
              reads=[RG.reg(), MASKC.reg()], writes=[bt.reg()])

    inproj_bank = [0]

    def inproj(n):
        jb = jobs[n]
        st, dil, kind = n % 2, jb["dil"], jb["kind"]
        stages = []
        for which in ("q", "k"):
            W = WQ[st] if which == "q" else WK[st]
            Wv_ = W.ap.rearrange("p (k c) -> p k c", c=128)
            dst = QT[st] if which == "q" else KT[st]
            if kind == "A":
                gvec = gqa if which == "q" else gka
            else:
                gvec = gqb if which == "q" else gkb
            for jbk in range(4):
                def main(which=which, Wv_=Wv_, W=W, jbk=jbk, dil=dil, kind=kind, gvec=gvec):
                    b = inproj_bank[0] % 2
                    inproj_bank[0] += 1
                    for k in range(16):
                        P.add("pe", lambda e, k=k, b=b: e.matmul(PS[b][:, :], lhsT=Wv_[:, k, :], rhs=XNv[:, k, jbk * 512:(jbk + 1) * 512],
                                                                  start=(k == 0), stop=(k == 15)),
                              reads=[W.reg(k * 128, (k + 1) * 128), xn_reg(k, jbk * 512, (jbk + 1) * 512)], writes=[psr(b)])
                    sq = SQ[b]
                    P.add("act", lambda e, sq=sq, b=b: e.activation(out=sq.ap, in_=PS[b][:, :], func=AF.Square),
                          reads=[psr(b)], writes=[sq.reg()])
                    if kind == "A":
                        q1 = Q1[b]
                        P.add("act", lambda e, q1=q1, b=b: e.activation(out=q1.ap, in_=PS[b][:, :], func=AF.Copy, scale=gvec),
                              reads=[psr(b), RCF], writes=[q1.reg()])
                    return b

                def post(b, which=which, dst=dst, jbk=jbk, dil=dil, kind=kind, gvec=gvec):
                    sq, rs = SQ[b], RS[b]
                    P.add("pe", lambda e, sq=sq: e.matmul(PS[2][:, :], lhsT=ones128, rhs=sq.ap, start=True, stop=True),
                          reads=[sq.reg(), RCB], writes=[psr(2)])
                    if dil == 1:
                        dreg = dst.reg(jbk * 512, (jbk + 1) * 512)
                        dap = dst.ap[:, jbk * 512:(jbk + 1) * 512]
                        pv = lambda ap: ap
                    else:
                        Mq, mb = S // dil, 512 // dil
                        dreg = dst.reg(dj=("qtw%d" % dst.off, jbk))
                        dap = dst.ap.rearrange("p (r m) -> p r m", m=Mq)[:, :, mb * jbk:mb * (jbk + 1)]
                        pv = lambda ap: ap.rearrange("p (m r) -> p r m", r=dil)
                    nsl = slice(jbk * 512, (jbk + 1) * 512)
                    if kind == "A":
                        q1, t1, t2 = Q1[b], T1[b], T2[b]
                        P.add("pe", lambda e, q1=q1: e.matmul(PS[3][:, :], lhsT=permT, rhs=q1.ap, start=True, stop=True),
                              reads=[q1.reg(), RCB], writes=[psr(3)])
                        P.add("dve", lambda e, t1=t1, b=b: e.scalar_tensor_tensor(
                            out=t1.ap, in0=PS[b][:, :], scalar=gvec, in1=ROPEv[:, 0, nsl], op0=ALU.mult, op1=ALU.mult),
                            reads=[psr(b), ROPE.reg(jbk * 512, (jbk + 1) * 512), RCF], writes=[t1.reg()])
                        rstd_ops(rs, 2)
                        P.add("dve", lambda e, t2=t2: e.tensor_tensor(out=t2.ap, in0=PS[3][:, :], in1=ROPEv[:, 1, nsl], op=ALU.mult),
                              reads=[psr(3), ROPE.reg(S + jbk * 512, S + (jbk + 1) * 512)], writes=[t2.reg()])
                        P.add("dve", lambda e, t1=t1, t2=t2: e.tensor_tensor(out=t1.ap, in0=t1.ap, in1=t2.ap, op=ALU.add),
                              reads=[t1.reg(), t2.reg()], writes=[t1.reg()])
                        P.add("dve", lambda e, t1=t1, rs=rs: e.tensor_tensor(out=dap, in0=pv(t1.ap), in1=pv(rs.ap), op=ALU.mult),
                              reads=[t1.reg(), rs.reg()], writes=[dreg])
                    else:
                        rstd_ops(rs, 2)
                        P.add("dve", lambda e, rs=rs, b=b: e.scalar_tensor_tensor(out=dap, in0=PS[b][:, :], scalar=gvec, in1=rs.ap, op0=ALU.mult, op1=ALU.mult),
                              reads=[psr(b), rs.reg(), RCF], writes=[dreg])
                stages.append((main, post))
        Wvv = WV[st].ap.rearrange("p (k c) -> p k c", c=128)
        Vb = VV[st]
        for grp in range(4):
            def mainv(grp=grp, Wvv=Wvv, Vb=Vb, dil=dil, st=st):
                b = inproj_bank[0] % 2
                inproj_bank[0] += 1
                for ti in range(4):
                    i = grp * 4 + ti
                    sl = tok_slice(dil, i)
                    for k in range(16):
                        P.add("pe", lambda e, k=k, b=b, ti=ti, sl=sl: e.matmul(PS[b][:, ti * 128:(ti + 1) * 128], lhsT=XNv[:, k, sl], rhs=Wvv[:, k, :],
                                                                          start=(k == 0), stop=(k == 15), skip_group_check=True),
                              reads=[WV[st].reg(k * 128, (k + 1) * 128), xn_reg(k)], writes=[psr(b)])
                P.add("act", lambda e, b=b, grp=grp: e.activation(out=Vb.ap[:, grp * 512:(grp + 1) * 512], in_=PS[b][:, :], func=AF.Copy),
                      reads=[psr(b)], writes=[Vb.reg(grp * 512, (grp + 1) * 512)])
                return b
            stages.append((mainv, lambda b: None))
        return stages

    def steps_of(stages):
        cells = [None] * len(stages)
        steps = []
        for i in range(len(stages) + 1):
            def f(i=i):
                if i < len(stages):
                    cells[i] = stages[i][0]()
                if i >= 1:
                    stages[i - 1][1](cells[i - 1])
            steps.append(f)
        return steps

    def merge_run(a, b):
        na, nb = len(a), len(b)
        i = j = 0
        while i < na or j < nb:
            if j >= nb or (i < na and (i + 0.5) / na <= (j + 0.5) / nb):
                a[i]()
                i += 1
            else:
                b[j]()
                j += 1

    od_count = [0]

    def attention(n):
        jb = jobs[n]
        st, dil, kind, j, g = n % 2, jb["dil"], jb["kind"], jb["j"], jb["g"]
        qt, kt, vb = QT[st], KT[st], VV[st]
        M = S // dil
        nt = M // 128
        stages = []
        ucount = [0]
        qblocks = []
        if kind == "A":
            for T in range(16):
                qi = T % nt
                base = T - qi
                kts = [(s_, base + qi - 1 + s_) for s_ in range(3) if 0 <= qi - 1 + s_ < nt]
                qblocks.append(dict(T=T, units=[kts]))
        else:
            for pb in range(16):
                if pb < 2:
                    tiles, off = [0, 1, 2, 3], 2 * pb
                elif pb >= 14:
                    tiles, off = [12, 13, 14, 15], 4 + 2 * (pb - 14)
                else:
                    tiles, off = [pb - 2 + c for c in range(5)], 4
                lo = tiles[0]
                v = 0 if 2 <= pb < 14 else 1
                u1 = [(tl - lo) for tl in tiles[:4]][::-1]
                units = [[(s_, lo + c) for s_, c in enumerate(u1)]]
                dd0s = [6 - 2 * u1[0] + off]
                if len(tiles) == 5:
                    units.append([(0, lo + 4)])
                    dd0s.append(6 - 2 * 4 + off)
                qblocks.append(dict(T=pb, units=units, v=v, dd0s=dd0s))
        for qb in qblocks:
            T = qb["T"]
            odi = od_count[0]
            od_count[0] += 1
            obank = 6 + (odi // 2) % 2
            ocol = (odi % 2) * 256
            first_in_bank = (odi % 2 == 0)
            nun = len(qb["units"])
            for ui, unit in enumerate(qb["units"]):
                def s_stage(unit=unit, T=T, qb=qb, ui=ui):
                    u = ucount[0]
                    ucount[0] += 1
                    sb_, pt = 4 + u % 2, PT[u % 2]
                    s0 = unit[0][0]
                    ns = len(unit)
                    for (s_, kti) in unit:
                        P.add("pe", lambda e, s_=s_, kti=kti, sb_=sb_: e.matmul(PS[sb_][:, s_ * 128:(s_ + 1) * 128], lhsT=kt.ap[:, kti * 128:(kti + 1) * 128],
                                                                            rhs=qt.ap[:, T * 128:(T + 1) * 128], start=True, stop=True, skip_group_check=True),
                              reads=[kt.reg(kti * 128, (kti + 1) * 128), qt.reg(T * 128, (T + 1) * 128)], writes=[psr(sb_)])
                    c0, c1 = s0 * 128, (s0 + ns) * 128
                    if kind == "A":
                        P.add("act", lambda e, sb_=sb_, pt=pt: e.activation(out=pt.ap[:, c0:c1], in_=PS[sb_][:, c0:c1], func=AF.Exp, scale=SCALE),
                              reads=[psr(sb_)], writes=[pt.reg(c0, c1)])
                        P.add("dve", lambda e, pt=pt: e.tensor_tensor(out=pt.ap[:, c0:c1], in0=pt.ap[:, c0:c1], in1=maskA[:, c0:c1], op=ALU.mult),
                              reads=[pt.reg(c0, c1), RCB], writes=[pt.reg(c0, c1)])
                    else:
                        bt, tmp = BT[n % 2], TMPB[u % 2]
                        v, dd0 = qb["v"], qb["dd0s"][ui]
                        b0 = v * 896 + dd0 * 64
                        P.add("dve", lambda e, sb_=sb_, tmp=tmp, bt=bt, b0=b0: e.scalar_tensor_tensor(
                            out=tmp.ap[:, c0:c1], in0=PS[sb_][:, c0:c1], scalar=SCALE, in1=bt.ap[:, b0:b0 + (c1 - c0)], op0=ALU.mult, op1=ALU.add),
                            reads=[psr(sb_), bt.reg()], writes=[tmp.reg(c0, c1)])
                        P.add("act", lambda e, tmp=tmp, pt=pt: e.activation(out=pt.ap[:, c0:c1], in_=tmp.ap[:, c0:c1], func=AF.Exp),
                              reads=[tmp.reg(c0, c1)], writes=[pt.reg(c0, c1)])
                    return (u, unit)

                def pv_stage(su, ui=ui, nun=nun, obank=obank, ocol=ocol, first_in_bank=first_in_bank, T=T, odi=odi):
                    u, unit = su
                    pt = PT[u % 2]
                    nk = len(unit)
                    for ii, (s_, kti) in enumerate(unit):
                        P.add("pe", lambda e, s_=s_, kti=kti, ii=ii: e.matmul(PS[obank][:, ocol:ocol + 128], lhsT=vb.ap[:, kti * 128:(kti + 1) * 128],
                                                                        rhs=pt.ap[:, s_ * 128:(s_ + 1) * 128],
                                                                        start=(first_in_bank and ui == 0 and ii == 0), stop=(ui == nun - 1 and ii == nk - 1), skip_group_check=True),
                              reads=[vb.reg(kti * 128, (kti + 1) * 128), pt.reg(s_ * 128, (s_ + 1) * 128)], writes=[psr(obank)])
                    for ii, (s_, kti) in enumerate(unit):
                        P.add("pe", lambda e, s_=s_, ii=ii: e.matmul(PS[obank][:, ocol + 128:ocol + 256], lhsT=ones1, rhs=pt.ap[:, s_ * 128:(s_ + 1) * 128],
                                                              start=False, stop=(ui == nun - 1 and ii == nk - 1), skip_group_check=True),
                              reads=[pt.reg(s_ * 128, (s_ + 1) * 128), RCB], writes=[psr(obank)])
                    if ui == nun - 1 and odi % 2 == 1:
                        for (TT, oc) in ((T - 1, 0), (T, 256)):
                            evac(TT, obank, oc)

                stages.append((s_stage, pv_stage))

        def evac(TT, obank, oc):
            Ops = PS[obank][:, oc:oc + 128]
            Dps = PS[obank][:, oc + 128:oc + 256]
            if kind == "B":
                rc = RCB2[TT % 2]
                P.add("dve", lambda e, rc=rc: e.reciprocal(out=rc.ap, in_=Dps), reads=[psr(obank)], writes=[rc.reg()])
                P.add("dve", lambda e, rc=rc: e.tensor_tensor(out=OABv[:, 4 + j, TT * 128:(TT + 1) * 128], in0=Ops, in1=rc.ap, op=ALU.mult),
                      reads=[psr(obank), rc.reg()], writes=[OAB.reg((4 + j) * S + TT * 128, (4 + j) * S + (TT + 1) * 128)])
                return
            sl = tok_slice(dil, TT)
            lo, hi = sl.start, sl.stop
            djk = ("acc%d" % dil, TT // nt) if dil > 1 else None
            accn = ACCN.ap[:, sl]
            accd = ACCD.ap[:, sl]
            rn, rd = ACCN.reg(lo, hi, djk), ACCD.reg(lo, hi, djk)
            if g == 0:
                P.add("act", lambda e: e.activation(out=accn, in_=Ops, func=AF.Copy), reads=[psr(obank)], writes=[rn])
                P.add("dve", lambda e: e.tensor_copy(out=accd, in_=Dps), reads=[psr(obank)], writes=[rd])
            elif g == 1:
                P.add("dve", lambda e: e.tensor_tensor(out=accn, in0=Ops, in1=accn, op=ALU.add), reads=[psr(obank), rn], writes=[rn])
                P.add("dve", lambda e: e.tensor_tensor(out=accd, in0=Dps, in1=accd, op=ALU.add), reads=[psr(obank), rd], writes=[rd])
            else:
                den, rc, num = DEN[TT % 2], RC[TT % 2], NUM[TT % 2]
                P.add("dve", lambda e: e.tensor_tensor(out=den.ap, in0=Dps, in1=accd, op=ALU.add), reads=[psr(obank), rd], writes=[den.reg()])
                P.add("dve", lambda e: e.tensor_tensor(out=num.ap, in0=Ops, in1=accn, op=ALU.add), reads=[psr(obank), rn], writes=[num.reg()])
                P.add("dve", lambda e: e.reciprocal(out=rc.ap, in_=den.ap), reads=[den.reg()], writes=[rc.reg()])
                P.add("dve", lambda e: e.tensor_tensor(out=OABv[:, j, sl], in0=num.ap, in1=rc.ap, op=ALU.mult),
                      reads=[num.reg(), rc.reg()], writes=[OAB.reg(j * S + lo, j * S + hi, djk)])
        return stages

    P.add("sp", lambda e: e.dma_start(out=ROPEv[:, :, :], in_=roped[:, :, :]), writes=[ROPE.reg()], sem="rope")
    load_head_weights(0)
    load_head_weights(1)
    emit_precast(8)
    merge_run(steps_of(inproj(0)), [])
    for n in range(16):
        merge_run(steps_of(inproj(n + 1)) if n + 1 < 16 else [], steps_of(attention(n)))
        if n == 11:
            P.add("sp", lambda e: e.dma_start(out=MASKC.ap, in_=maskcd[:, :]), writes=[MASKC.reg()], sem="maskc")
        if n + 1 < 16 and jobs[n + 1]["kind"] == "B":
            setup_tables(n + 1)
        if n + 2 < 16:
            load_head_weights(n + 2)
        emit_precast(5)

    if debug:
        P.add("sp", lambda e: e.dma_start(out=dbg_xn[:, :], in_=XN.ap), reads=[XN.reg()], sem="dbg")
        P.add("sp", lambda e: e.dma_start(out=dbg_oab[:, :], in_=OAB.ap), reads=[OAB.reg()], sem="dbg")

    MIX = Buf(TB0, 65536, BF16)
    MIXv = MIX.ap.rearrange("p (tb k t) -> p tb k t", k=16, t=512)
    o = TB0 + 65536
    WGA = [Buf(o + i * 8192, 4096, BF16) for i in range(2)]
    WGB = [Buf(o + i * 8192 + 4096, 4096, BF16) for i in range(2)]
    o += 16384
    PA = [Buf(o + i * 2048, 1024, BF16) for i in range(2)]
    PBb = [Buf(o + i * 2048 + 1024, 1024, BF16) for i in range(2)]
    o += 4096
    SA = [Buf(o + i * 8192, 2048, F32) for i in range(2)]
    SBg = [Buf(o + i * 8192 + 2048, 2048, F32) for i in range(2)]
    M1 = [Buf(o + i * 8192 + 4096, 2048, F32) for i in range(2)]
    M2 = [Buf(o + i * 8192 + 6144, 2048, F32) for i in range(2)]
    w_pa_v = w_pa.rearrange("(s p) c -> p s c", p=128)
    w_pb_v = w_pb.rearrange("(s p) c -> p s c", p=128)

    def load_c1(c):
        st = c % 2
        P.add("pool", lambda e: e.dma_start(out=WGA[st].ap.rearrange("p (k c) -> p k c", c=128), in_=w_in_v[:, :, 6144 + c * 128:6144 + (c + 1) * 128]),
              writes=[WGA[st].reg()], sem="wg%d" % st)
        P.add("pool", lambda e: e.dma_start(out=WGB[st].ap.rearrange("p (k c) -> p k c", c=128), in_=w_in_v[:, :, 8192 + c * 128:8192 + (c + 1) * 128]),
              writes=[WGB[st].reg()], sem="wg%d" % st)
        P.add("pool", lambda e: e.dma_start(out=PA[st].ap.rearrange("p (s c) -> p s c", c=128), in_=w_pa_v[:, :, c * 128:(c + 1) * 128]),
              writes=[PA[st].reg()], sem="wg%d" % st)
        P.add("pool", lambda e: e.dma_start(out=PBb[st].ap.rearrange("p (s c) -> p s c", c=128), in_=w_pb_v[:, :, c * 128:(c + 1) * 128]),
              writes=[PBb[st].reg()], sem="wg%d" % st)

    load_c1(0)
    load_c1(1)
    cnt = 0
    for c in range(16):
        st = c % 2
        wga = WGA[st].ap.rearrange("p (k c) -> p k c", c=128)
        wgb = WGB[st].ap.rearrange("p (k c) -> p k c", c=128)
        pa = PA[st].ap.rearrange("p (s c) -> p s c", c=128)
        pbm = PBb[st].ap.rearrange("p (s c) -> p s c", c=128)
        for tb in range(4):
            bs = 4 * (cnt % 2)
            ss = cnt % 2
            cnt += 1
            tsl = slice(tb * 512, (tb + 1) * 512)
            for k in range(16):
                P.add("pe", lambda e, k=k, bs=bs, wga=wga, tsl=tsl: e.matmul(PS[bs][:, :], lhsT=wga[:, k, :], rhs=XNv[:, k, tsl], start=(k == 0), stop=(k == 15)),
                      reads=[WGA[st].reg(), xn_reg(k, tb * 512, (tb + 1) * 512)], writes=[psr(bs)])
            for k in range(16):
                P.add("pe", lambda e, k=k, bs=bs, wgb=wgb, tsl=tsl: e.matmul(PS[bs + 1][:, :], lhsT=wgb[:, k, :], rhs=XNv[:, k, tsl], start=(k == 0), stop=(k == 15)),
                      reads=[WGB[st].reg(), xn_reg(k, tb * 512, (tb + 1) * 512)], writes=[psr(bs + 1)])
            for s_ in range(4):
                P.add("pe", lambda e, s_=s_, bs=bs, pa=pa, tsl=tsl: e.matmul(PS[bs + 2][:, :], lhsT=pa[:, s_, :], rhs=OABv[:, s_, tsl], start=(s_ == 0), stop=(s_ == 3)),
                      reads=[PA[st].reg(), OAB.reg(s_ * S + tb * 512, s_ * S + (tb + 1) * 512)], writes=[psr(bs + 2)])
            for s_ in range(4):
                P.add("pe", lambda e, s_=s_, bs=bs, pbm=pbm, tsl=tsl: e.matmul(PS[bs + 3][:, :], lhsT=pbm[:, s_, :], rhs=OABv[:, 4 + s_, tsl], start=(s_ == 0), stop=(s_ == 3)),
                      reads=[PBb[st].reg(), OAB.reg((4 + s_) * S + tb * 512, (4 + s_) * S + (tb + 1) * 512)], writes=[psr(bs + 3)])
            sa, sb2, m1, m2 = SA[ss], SBg[ss], M1[ss], M2[ss]
            P.add("act", lambda e, sa=sa, bs=bs, c=c: e.activation(out=sa.ap, in_=PS[bs][:, :], func=AF.Sigmoid, bias=bgc(c), scale=1.0),
                  reads=[psr(bs), RCF], writes=[sa.reg()])
            P.add("act", lambda e, sb2=sb2, bs=bs, c=c: e.activation(out=sb2.ap, in_=PS[bs + 1][:, :], func=AF.Sigmoid, bias=bgc(16 + c), scale=1.0),
                  reads=[psr(bs + 1), RCF], writes=[sb2.reg()])
            P.add("dve", lambda e, m1=m1, sa=sa, bs=bs: e.tensor_tensor(out=m1.ap, in0=PS[bs + 2][:, :], in1=sa.ap, op=ALU.mult),
                  reads=[psr(bs + 2), sa.reg()], writes=[m1.reg()])
            P.add("dve", lambda e, m2=m2, sb2=sb2, bs=bs: e.tensor_tensor(out=m2.ap, in0=PS[bs + 3][:, :], in1=sb2.ap, op=ALU.mult),
                  reads=[psr(bs + 3), sb2.reg()], writes=[m2.reg()])
            mo = (tb * 16 + c) * 512
            P.add("dve", lambda e, m1=m1, m2=m2, tb=tb, c=c: e.tensor_tensor(out=MIXv[:, tb, c, :], in0=m1.ap, in1=m2.ap, op=ALU.add),
                  reads=[m1.reg(), m2.reg()], writes=[MIX.reg(mo, mo + 512)])
        if c + 2 < 16:
            load_c1(c + 2)
        emit_precast(4)
    emit_precast(1000)
    if debug:
        P.add("sp", lambda e: e.dma_start(out=dbg_mix[:, :], in_=MIX.ap), reads=[MIX.reg()], sem="dbg")

    U2 = Buf(0, 65536, BF16)
    U2v = U2.ap.rearrange("p (f t) -> p f t", t=512)
    Hb = Buf(65536, 32768, F32)
    Hv = Hb.ap.rearrange("p (c t) -> p c t", t=512)
    o = TB0 + 65536
    HN = Buf(o, 16384, BF16)
    HNv = HN.ap.rearrange("p (k t) -> p k t", t=512)
    o += 16384
    NRING = 4
    RING = [Buf(o + i * 4096, 4096, BF16) for i in range(NRING)]
    o += NRING * 4096
    TT_ = [Buf(o + i * 2048, 2048, F32) for i in range(2)]
    o += 4096
    SQD = [Buf(o + i * 1024, 1024, BF16) for i in range(2)]
    o += 2048
    R2 = Buf(o, 2048, F32)

    stream = []
    for tb in range(4):
        for c in range(16):
            stream.append((s_out[c], Reg("s_out", c, c + 1)))
        for f in range(64):
            stream.append((s_up[f], Reg("s_up", f, f + 1)))
        for cp in range(8):
            for fg in range(4):
                for c in (2 * cp, 2 * cp + 1):
                    t = c * 4 + fg
                    stream.append((s_dn[t], Reg("s_dn", t, t + 1)))
    spos = [0]
    PREF = 3

    def ring_load_upto(i):
        while spos[0] <= min(i, len(stream) - 1):
            n_ = spos[0]
            src, rg = stream[n_]
            rb = RING[n_ % NRING]
            P.add("sp", lambda e, src=src, rb=rb: e.dma_start(out=rb.ap, in_=src), reads=[rg], writes=[rb.reg()], sem="ring%d" % (n_ % NRING))
            spos[0] += 1

    tix = [0]

    def next_tile():
        n_ = tix[0]
        tix[0] += 1
        ring_load_upto(n_ + PREF)
        rb = RING[n_ % NRING]
        return rb, rb.ap.rearrange("p (k c) -> p k c", c=128)

    def xload(tb, c):
        P.add("sp", lambda e: e.dma_start(out=Hv[:, c, :], in_=xTv[:, c, tb * 512:(tb + 1) * 512]),
              writes=[Hb.reg(c * 512, (c + 1) * 512)], sem="hx")

    upbank = [0]
    outs = []
    for tb in range(4):
        xload(tb, 0)
        xload(tb, 1)
        pend = None
        for c in range(16):
            if c + 2 < 16:
                xload(tb, c + 2)
            rb, rv = next_tile()
            b = upbank[0] % 2
            upbank[0] += 1
            for k in range(16):
                P.add("pe", lambda e, k=k, b=b, rv=rv, tb=tb: e.matmul(PS[b][:, :], lhsT=rv[:, k, :], rhs=MIXv[:, tb, k, :], start=(k == 0), stop=(k == 15)),
                      reads=[rb.reg(), MIX.reg((tb * 16 + k) * 512, (tb * 16 + k + 1) * 512)], writes=[psr(b)])
            hreg = Hb.reg(c * 512, (c + 1) * 512)
            P.add("dve", lambda e, b=b, c=c: e.tensor_tensor(out=Hv[:, c, :], in0=PS[b][:, :], in1=Hv[:, c, :], op=ALU.add),
                  reads=[psr(b), hreg], writes=[hreg])
            P.add("act", lambda e, c=c: e.activation(out=HNv[:, c, :], in_=Hv[:, c, :], func=AF.Copy, scale=g2(c)),
                  reads=[hreg, RCF], writes=[HN.reg(c * 512, (c + 1) * 512)])
            sq = SQD[c % 2]
            P.add("act", lambda e, c=c, sq=sq: e.activation(out=sq.ap, in_=Hv[:, c, :], func=AF.Square),
                  reads=[hreg], writes=[sq.reg()])
            if pend is not None:
                pend()
            pend = (lambda sq=sq, c=c: P.add("pe", lambda e: e.matmul(PS[2][:, :], lhsT=ones2048, rhs=sq.ap, start=(c == 0), stop=(c == 15)),
                                             reads=[sq.reg(), RCB], writes=[psr(2)]))
        pend()
        rstd_ops(R2, 2)
        for f in range(64):
            rb, rv = next_tile()
            b = upbank[0] % 2
            upbank[0] += 1
            for k in range(16):
                P.add("pe", lambda e, k=k, b=b, rv=rv: e.matmul(PS[b][:, :], lhsT=rv[:, k, :], rhs=HNv[:, k, :], start=(k == 0), stop=(k == 15)),
                      reads=[rb.reg(), HN.reg(k * 512, (k + 1) * 512)], writes=[psr(b)])
            tt = TT_[f % 2]
            P.add("dve", lambda e, b=b, tt=tt: e.scalar_tensor_tensor(out=tt.ap, in0=PS[b][:, :], scalar=0.0, in1=R2.ap, op0=ALU.max, op1=ALU.mult),
                  reads=[psr(b), R2.reg()], writes=[tt.reg()])
            P.add("act", lambda e, f=f, tt=tt: e.activation(out=U2v[:, f, :], in_=tt.ap, func=AF.Square),
                  reads=[tt.reg()], writes=[U2.reg(f * 512, (f + 1) * 512)])
        for cp in range(8):
            banks = (3, 4) if cp % 2 == 0 else (5, 6)
            for fg in range(4):
                for ci, c in enumerate((2 * cp, 2 * cp + 1)):
                    rb, rv = next_tile()
                    bk = banks[ci]
                    for kk in range(16):
                        f = fg * 16 + kk
                        P.add("pe", lambda e, kk=kk, f=f, bk=bk, rv=rv, fg=fg: e.matmul(PS[bk][:, :], lhsT=rv[:, kk, :], rhs=U2v[:, f, :],
                                                                              start=(fg == 0 and kk == 0), stop=(fg == 3 and kk == 15)),
                              reads=[rb.reg(), U2.reg(f * 512, (f + 1) * 512)], writes=[psr(bk)])
            for ci, c in enumerate((2 * cp, 2 * cp + 1)):
                bk = banks[ci]
                hreg = Hb.reg(c * 512, (c + 1) * 512)
                P.add("dve", lambda e, bk=bk, c=c: e.tensor_tensor(out=Hv[:, c, :], in0=PS[bk][:, :], in1=Hv[:, c, :], op=ALU.add),
                      reads=[psr(bk), hreg], writes=[hreg])
                outs.append(P.add("sp", lambda e, c=c, tb=tb: e.dma_start(out=outTv[:, c, tb * 512:(tb + 1) * 512], in_=Hv[:, c, :]),
                                  reads=[hreg], sem="ost"))
    final_sems = ["ost"] + (["dbg"] if debug else [])

    from contextlib import ExitStack
    with ExitStack() as es:
        es.enter_context(nc.allow_low_precision("bf16 matmul operands, fp32 accumulation"))
        esems = {e: es.enter_context(nc.semaphore("e_" + e)) for e in ("pe", "act", "dve")}
        dsems = {name: es.enter_context(nc.semaphore("d_" + name)) for name in P.dsem}
        block = es.enter_context(nc.Block())
        run = P.emit(esems, dsems)

        def mk(ename):
            def f(eng):
                run(ename, eng)
                if ename == "sp":
                    for sname in final_sems:
                        eng.wait_ge(dsems[sname], P.dsem[sname])
            return f
        block.tensor(mk("pe"))
        block.scalar(mk("act"))
        block.vector(mk("dve"))
        block.gpsimd(mk("pool"))
        block.sync(mk("sp"))
    return nc


_CACHE = {}


def _prep_inputs(inp):
    cb, rope, maskc = _const_tables()
    f32 = np.float32
    cvec = np.zeros((128, 69), f32)
    cvec[:, 68] = EPS
    cvec[:, 0:16] = np.asarray(inp["norm_mix"], f32).reshape(16, 128).T
    cvec[:, 16:32] = np.asarray(inp["norm_ffn"], f32).reshape(16, 128).T
    cvec[:, 32:64] = np.asarray(inp["b_gate"], f32).reshape(32, 128).T
    cvec[:, 64] = np.asarray(inp["q_norm_a"], f32).reshape(128)
    cvec[:, 65] = np.asarray(inp["k_norm_a"], f32).reshape(128)
    cvec[:, 66] = np.asarray(inp["q_norm_b"], f32).reshape(128)
    cvec[:, 67] = np.asarray(inp["k_norm_b"], f32).reshape(128)
    shared = {
        "w_in": np.ascontiguousarray(np.asarray(inp["w_in"], f32)[0]),
        "w_pa": np.ascontiguousarray(np.asarray(inp["w_proj_a"], f32)[0]),
        "w_pb": np.ascontiguousarray(np.asarray(inp["w_proj_b"], f32)[0]),
        "w_out": np.ascontiguousarray(np.asarray(inp["w_out"], f32)[0]),
        "w_up": np.ascontiguousarray(np.asarray(inp["w_up"], f32)[0]),
        "w_dn": np.ascontiguousarray(np.asarray(inp["w_down"], f32)[0]),
        "cvec": cvec, "cb": cb, "rope": rope, "maskc": maskc,
        "rpbg": _rpb_gather(np.asarray(inp["rpb_b"], f32)[0]),
    }
    x = np.asarray(inp["x"], f32)
    in_maps = []
    for b in range(NCORES):
        m = dict(shared)
        m["xT"] = np.ascontiguousarray(x[b].T)
        in_maps.append(m)
    return in_maps


def kernel(**inputs):
    if "nc" not in _CACHE:
        _CACHE["nc"] = build(debug=False)
    nc = _CACHE["nc"]
    in_maps = _prep_inputs(inputs)
    res = run_bass_kernel_spmd(nc, in_maps, core_ids=list(range(NCORES)))
    out = np.stack([np.ascontiguousarray(res.results[b]["outT"].T) for b in range(NCORES)], axis=0)
    return out.astype(np.float32)
```

```python
import numpy as np
import ml_dtypes
import concourse.bass as bass
import concourse.mybir as mybir
from concourse.bass_utils import run_bass_kernel_spmd

F32 = mybir.dt.float32
BF16 = mybir.dt.bfloat16
AF = mybir.ActivationFunctionType
ALU = mybir.AluOpType

S = 2048
D = 2048
NCORES = 8
EPS = 1e-6
SCALE = 128.0 ** -0.5
NEG = -30000.0
ENGS = ("pe", "act", "dve", "pool", "sp")


class Op:
    __slots__ = ("eng", "fn", "idx", "waits", "signaled", "dma_sem", "dma_val")


class Reg:
    __slots__ = ("arena", "lo", "hi", "dj")

    def __init__(self, arena, lo, hi, dj=None):
        self.arena, self.lo, self.hi, self.dj = arena, lo, hi, dj


class Prog:
    def __init__(self, nc):
        self.nc = nc
        self.q = {e: [] for e in ENGS}
        self.known = {e: {} for e in ENGS}
        self.recs = {}
        self.excl = set()
        self.dsem = {}

    def _overl(self, reg):
        out = []
        for rec in self.recs.get(reg.arena, ()):
            if rec[0] < reg.hi and reg.lo < rec[1]:
                d0 = rec[2]
                if d0 is not None and reg.dj is not None and d0[0] == reg.dj[0] and d0[1] != reg.dj[1]:
                    continue
                out.append(rec)
        return out

    def add(self, eng, fn, reads=(), writes=(), sem=None):
        op = Op()
        op.eng, op.fn, op.idx, op.signaled = eng, fn, len(self.q[eng]), False
        op.dma_sem, op.dma_val = sem, 0
        deps = {}

        def need(o):
            if o is None:
                return
            if o.dma_sem is not None:
                key, val = ("s", o.dma_sem), self.dsem[o.dma_sem]
            else:
                if o.eng == "pe" and eng == "pe":
                    return
                key, val = ("e", o.eng), o.idx
            if deps.get(key, -1) < val:
                deps[key] = val

        racc, wacc = [], []
        for r in reads:
            (wacc if r.arena in self.excl else racc).append(r)
        wacc.extend(writes)
        for r in racc:
            for rec in self._overl(r):
                need(rec[3])
        for r in wacc:
            for rec in self._overl(r):
                need(rec[3])
                for o in rec[4].values():
                    need(o)
        waits = []
        kn = self.known[eng]
        for key, val in deps.items():
            if kn.get(key, -1) >= val:
                continue
            kn[key] = val
            waits.append((key, val))
            if key[0] == "e":
                self.q[key[1]][val].signaled = True
        op.waits = waits
        if sem is not None:
            self.dsem[sem] = self.dsem.get(sem, 0) + 16
            op.dma_val = self.dsem[sem]
        for r in racc:
            ov = self._overl(r)
            contained = False
            for rec in ov:
                rec[4][eng if sem is None else ("d", sem, op.idx)] = op
                if rec[0] <= r.lo and r.hi <= rec[1]:
                    contained = True
            if not contained:
                self.recs.setdefault(r.arena, []).append([r.lo, r.hi, r.dj, None, {(eng if sem is None else ("d", sem, op.idx)): op}])
        for r in wacc:
            lst = self.recs.setdefault(r.arena, [])
            ov = self._overl(r)
            for rec in ov:
                if r.lo <= rec[0] and rec[1] <= r.hi:
                    lst.remove(rec)
            lst.append([r.lo, r.hi, r.dj, op, {}])
        self.q[eng].append(op)
        return op

    def emit(self, esems, dsems):
        tick = {}
        for e in ENGS:
            c = 0
            t = []
            for op in self.q[e]:
                if op.signaled:
                    c += 1
                t.append(c)
            tick[e] = t
        self.tick = tick

        def run(e, eng):
            for op in self.q[e]:
                for key, val in op.waits:
                    if key[0] == "e":
                        eng.wait_ge(esems[key[1]], tick[key[1]][val])
                    else:
                        eng.wait_ge(dsems[key[1]], val)
                ins = op.fn(eng)
                if op.dma_sem is not None:
                    ins.then_inc(dsems[op.dma_sem], 16)
                elif op.signaled:
                    ins.then_inc(esems[e], 1)
        return run


def _const_tables():
    bf = ml_dtypes.bfloat16
    cb = np.zeros((128, 7, 128), np.float32)
    cb[:, 0, :] = 1.0 / 2048.0
    cb[:, 1, :] = 1.0 / 128.0
    cb[:, 2, :] = 1.0
    ein = np.arange(128)[:, None]
    eout = np.arange(128)[None, :]
    cb[:, 3, :] = (ein == (eout + 64) % 128)
    kk = np.arange(128)[:, None]
    qq = np.arange(128)[None, :]
    cb[:, 4, :] = (kk - qq >= 64)
    cb[:, 5, :] = (np.abs(kk - qq) <= 64)
    cb[:, 6, :] = (qq - kk >= 64)
    cb = cb.reshape(128, 7 * 128).astype(bf)
    pos = np.arange(S, dtype=np.float32)
    inv = (np.float32(10000.0) ** (-np.arange(0, 128, 2, dtype=np.float32) / np.float32(128))).astype(np.float32)
    ang = (pos[:, None] * inv[None, :]).astype(np.float32)
    cos = np.cos(ang).astype(np.float32).T
    sin = np.sin(ang).astype(np.float32).T
    cosT = np.concatenate([cos, cos], axis=0)
    sinT = np.concatenate([-sin, sin], axis=0)
    rope = np.ascontiguousarray(np.stack([cosT, sinT], axis=1))
    maskc = np.zeros((128, 2, 14, 64), np.float32)
    cq = np.arange(64)
    cstart = np.clip(cq - 8, 0, 48)
    for jj in range(2):
        for ck in range(64):
            p = jj * 64 + ck
            colv = (ck >= cstart) & (ck < cstart + 16)
            for v in range(2):
                for dd in range(14):
                    e = 13 - dd + jj
                    ok = colv & ((v == 1) or (3 <= e <= 10))
                    maskc[p, v, dd, :] = np.where(ok, 0.0, NEG)
    return cb, rope, maskc.reshape(128, 2 * 14 * 64)


def _rpb_gather(rpb):
    jj = np.arange(2)[:, None, None, None]
    ck = np.arange(64)[None, :, None, None]
    dd = np.arange(14)[None, None, :, None]
    cq = np.arange(64)[None, None, None, :]
    e = np.broadcast_to(13 - dd + jj, (2, 64, 14, 64))
    dc = np.broadcast_to(np.clip(ck - cq, -15, 15) + 15, (2, 64, 14, 64))
    g = rpb[:, e, dc]
    g = g.reshape(4, 128, 1, 14, 64)
    g = np.broadcast_to(g, (4, 128, 2, 14, 64))
    return np.ascontiguousarray(g).reshape(4, 128, 2 * 14 * 64).astype(np.float32)


SB_BYTES = 208896


def build(debug=False):
    nc = bass.Bass("TRN2", target_bir_lowering=False)
    P = Prog(nc)

    def dram(name, shape, dt, kind="ExternalInput"):
        return nc.dram_tensor(name, list(shape), dt, kind=kind).ap()

    xT = dram("xT", [D, S], F32)
    w_in = dram("w_in", [D, 10240], F32)
    w_pa = dram("w_pa", [512, D], F32)
    w_pb = dram("w_pb", [512, D], F32)
    w_out = dram("w_out", [D, D], F32)
    w_up = dram("w_up", [D, 8192], F32)
    w_dn = dram("w_dn", [8192, D], F32)
    cvec = dram("cvec", [128, 69], F32)
    cbd = dram("cb", [128, 896], BF16)
    roped = dram("rope", [128, 2, S], F32)
    maskcd = dram("maskc", [128, 1792], F32)
    rpbg = dram("rpbg", [4, 128, 1792], F32)
    outT = dram("outT", [D, S], F32, kind="ExternalOutput")
    s_out = dram("s_out", [16, 128, 2048], BF16, kind="Internal")
    s_up = dram("s_up", [64, 128, 2048], BF16, kind="Internal")
    s_dn = dram("s_dn", [64, 128, 2048], BF16, kind="Internal")
    if debug:
        dbg_xn = dram("dbg_xn", [128, 16 * S], BF16, kind="ExternalOutput")
        dbg_oab = dram("dbg_oab", [128, 8 * S], BF16, kind="ExternalOutput")
        dbg_mix = dram("dbg_mix", [128, 4 * 16 * 512], BF16, kind="ExternalOutput")

    SBt = nc.alloc_sbuf_tensor("SB", [128, SB_BYTES // 2], BF16).ap()
    CF = nc.alloc_sbuf_tensor("CF", [128, 69], F32).ap()
    CB = nc.alloc_sbuf_tensor("CBt", [128, 896], BF16).ap()
    PS = [nc.alloc_psum_tensor("ps%d" % i, [128, 512], F32).ap() for i in range(8)]
    for i in range(8):
        P.excl.add("PS%d" % i)

    def psr(i):
        return Reg("PS%d" % i, 0, 2048)

    class Buf:
        def __init__(self, off, nbytes, dt):
            self.off, self.nbytes, self.dt = off, nbytes, dt
            assert off % 4 == 0 and off + nbytes <= SB_BYTES, (off, nbytes)
            a = SBt[:, off // 2:(off + nbytes) // 2]
            self.ap = a.bitcast(F32) if dt == F32 else a
            self.esz = 4 if dt == F32 else 2

        def reg(self, lo=None, hi=None, dj=None):
            if lo is None:
                return Reg("SB", self.off, self.off + self.nbytes, dj)
            return Reg("SB", self.off + lo * self.esz, self.off + hi * self.esz, dj)

    def creg(name):
        return Reg(name, 0, 1)

    g1 = lambda k: CF[:, k:k + 1]
    g2 = lambda k: CF[:, 16 + k:17 + k]
    bgc = lambda c: CF[:, 32 + c:33 + c]
    gqa, gka, gqb, gkb = CF[:, 64:65], CF[:, 65:66], CF[:, 66:67], CF[:, 67:68]
    ones2048 = CB[:, 0:128]
    ones128 = CB[:, 128:256]
    ones1 = CB[:, 256:384]
    permT = CB[:, 384:512]
    maskA = CB[:, 512:896]

    epsc = CF[:, 68:69]

    def rstd_ops(dst, pbank):
        P.add("act", lambda e: e.activation(out=dst.ap, in_=PS[pbank][:, :], func=AF.Ln, bias=epsc, scale=1.0),
              reads=[psr(pbank), RCF], writes=[dst.reg()])
        P.add("act", lambda e: e.activation(out=dst.ap, in_=dst.ap, func=AF.Exp, scale=-0.5), reads=[dst.reg()], writes=[dst.reg()])

    P.add("sp", lambda e: e.dma_start(out=CF[:, :], in_=cvec[:, :]), writes=[creg("CF")], sem="const")
    P.add("sp", lambda e: e.dma_start(out=CB[:, :], in_=cbd[:, :]), writes=[creg("CB")], sem="const")
    RCF, RCB = creg("CF"), creg("CB")

    xTv = xT.rearrange("(k p) t -> p k t", p=128)
    outTv = outT.rearrange("(k p) t -> p k t", p=128)
    w_in_v = w_in.rearrange("(k p) c -> p k c", p=128)

    XN = Buf(0, 65536, BF16)
    XNv = XN.ap.rearrange("p (k t) -> p k t", t=S)
    OAB = Buf(65536, 32768, BF16)
    OABv = OAB.ap.rearrange("p (j t) -> p j t", t=S)
    TB0 = 98304

    def xn_reg(k, t0=0, t1=S):
        return XN.reg(k * S + t0, k * S + t1)

    precast_jobs = []
    w_out_v = w_out.rearrange("(k p) c -> p k c", p=128)
    w_up_v = w_up.rearrange("(k p) c -> p k c", p=128)
    w_dn_v = w_dn.rearrange("(g k p) c -> g p k c", p=128, k=16)
    for c in range(16):
        precast_jobs.append((s_out[c].rearrange("p (k c) -> p k c", c=128), w_out_v[:, :, c * 128:(c + 1) * 128], Reg("s_out", c, c + 1)))
    for f in range(64):
        precast_jobs.append((s_up[f].rearrange("p (k c) -> p k c", c=128), w_up_v[:, :, f * 128:(f + 1) * 128], Reg("s_up", f, f + 1)))
    for c in range(16):
        for fg in range(4):
            t = c * 4 + fg
            precast_jobs.append((s_dn[t].rearrange("p (k c) -> p k c", c=128), w_dn_v[fg][:, :, c * 128:(c + 1) * 128], Reg("s_dn", t, t + 1)))
    precast_pos = [0]

    def emit_precast(n, after=()):
        for _ in range(n):
            if precast_pos[0] >= len(precast_jobs):
                return
            dst, src, rg = precast_jobs[precast_pos[0]]
            precast_pos[0] += 1
            P.add("pool", lambda e, dst=dst, src=src: e.dma_start(out=dst, in_=src), reads=list(after), writes=[rg], sem="precast")

    XS = [Buf(TB0 + 24576, 32768, F32), Buf(TB0 + 24576 + 32768, 32768, F32)]
    SQA = [Buf(TB0 + 24576 + 65536 + i * 1024, 1024, BF16) for i in range(4)]
    R1 = [Buf(TB0 + 24576 + 65536 + 4096 + i * 2048, 2048, F32) for i in range(2)]
    for tb in range(4):
        xs = XS[tb % 2]
        xsv = xs.ap.rearrange("p (k t) -> p k t", t=512)
        for j in range(4):
            P.add("sp", lambda e, xsv=xsv, j=j, tb=tb: e.dma_start(out=xsv[:, 4 * j:4 * j + 4, :], in_=xTv[:, 4 * j:4 * j + 4, tb * 512:(tb + 1) * 512]),
                  writes=[xs.reg(j * 2048, (j + 1) * 2048)], sem="xs%d" % (tb % 2))
        pb = tb % 2
        for k in range(16):
            sq = SQA[k % 4]
            P.add("act", lambda e, sq=sq, xsv=xsv, k=k: e.activation(out=sq.ap, in_=xsv[:, k, :], func=AF.Square),
                  reads=[xs.reg(k * 512, (k + 1) * 512)], writes=[sq.reg()])
            P.add("pe", lambda e, sq=sq, k=k, pb=pb: e.matmul(PS[pb][:, :], lhsT=ones2048, rhs=sq.ap, start=(k == 0), stop=(k == 15)),
                  reads=[sq.reg(), RCB], writes=[psr(pb)])
        r1 = R1[tb % 2]
        rstd_ops(r1, pb)
        for k in range(16):
            P.add("dve", lambda e, k=k, tb=tb, xsv=xsv, r1=r1: e.scalar_tensor_tensor(
                out=XNv[:, k, tb * 512:(tb + 1) * 512], in0=xsv[:, k, :], scalar=g1(k), in1=r1.ap, op0=ALU.mult, op1=ALU.mult),
                reads=[xs.reg(k * 512, (k + 1) * 512), r1.reg(), RCF], writes=[xn_reg(k, tb * 512, (tb + 1) * 512)])

    o = TB0
    WQ = [Buf(o + i * 12288, 4096, BF16) for i in range(2)]
    WK = [Buf(o + i * 12288 + 4096, 4096, BF16) for i in range(2)]
    WV = [Buf(o + i * 12288 + 8192, 4096, BF16) for i in range(2)]
    o += 24576
    QT = [Buf(o + i * 12288, 4096, BF16) for i in range(2)]
    KT = [Buf(o + i * 12288 + 4096, 4096, BF16) for i in range(2)]
    VV = [Buf(o + i * 12288 + 8192, 4096, BF16) for i in range(2)]
    o += 24576
    SQ = [Buf(o + i * 1024, 1024, BF16) for i in range(2)]
    Q1 = [Buf(o + 2048 + i * 1024, 1024, BF16) for i in range(2)]
    o += 4096
    RS = [Buf(o + i * 2048, 2048, F32) for i in range(2)]
    o += 4096
    T1 = [Buf(o + i * 2048, 2048, F32) for i in range(2)]
    T2 = [Buf(o + 4096 + i * 2048, 2048, F32) for i in range(2)]
    o += 8192
    PT = [Buf(o + i * 1024, 1024, BF16) for i in range(2)]
    o += 2048
    AO = o
    ACCN = Buf(AO, 8192, F32)
    ACCD = Buf(AO + 8192, 8192, F32)
    ROPE = Buf(AO + 16384, 16384, F32)
    ROPEv = ROPE.ap.rearrange("p (c t) -> p c t", t=S)
    RC = [Buf(AO + 32768 + i * 512, 512, F32) for i in range(2)]
    DEN = [Buf(AO + 32768 + 1024 + i * 512, 512, F32) for i in range(2)]
    NUM = [Buf(AO + 32768 + 2048 + i * 512, 512, F32) for i in range(2)]
    MASKC = Buf(AO, 7168, F32)
    RG = Buf(AO + 7168, 7168, F32)
    BT = [Buf(AO + 14336 + i * 7168, 7168, F32) for i in range(2)]
    TMPB = [Buf(AO + 28672 + i * 2048, 2048, F32) for i in range(2)]
    RCB2 = [Buf(AO + 32768 + i * 512, 512, F32) for i in range(2)]
    TMPB.append(Buf(AO + 32768 + 1024, 2048, F32))
    PT3 = [PT[0], PT[1], Buf(AO + 32768 + 1024 + 2048, 1024, BF16)]

    jobs = []
    for j in range(4):
        for g, dil in enumerate((1, 4, 16)):
            jobs.append(dict(h=4 * g + j, kind="A", dil=dil, j=j, g=g))
    for j in range(4):
        jobs.append(dict(h=12 + j, kind="B", dil=1, j=j, g=0))

    def tok_slice(dil, i):
        M = S // dil
        nt = M // 128
        s_, m0 = i // nt, (i % nt) * 128
        st = s_ + dil * m0
        return slice(st, st + dil * 127 + 1, dil)

    def blk_view(ap2d, dil, jb):
        if dil == 1:
            return ap2d[:, jb * 512:(jb + 1) * 512]
        if dil == 4:
            return ap2d[:, jb:jb + 4 * 511 + 1:4]
        return ap2d.rearrange("p (m r) -> p r m", r=16)[:, 4 * jb:4 * jb + 4, :]

    def shp(ap, dil):
        if dil == 16:
            return ap.rearrange("p (a b) -> p a b", b=128)
        return ap

    def load_head_weights(n):
        jb = jobs[n]
        h, st = jb["h"], n % 2
        for (buf, c0) in ((WQ[st], h * 128), (WK[st], 2048 + h * 128), (WV[st], 4096 + h * 128)):
            P.add("pool", lambda e, buf=buf, c0=c0: e.dma_start(out=buf.ap.rearrange("p (k c) -> p k c", c=128), in_=w_in_v[:, :, c0:c0 + 128]),
                  writes=[buf.reg()], sem="wset%d" % st)

    def setup_tables(n):
        jb = jobs[n]
        bt = BT[n % 2]
        P.add("sp", lambda e, jb=jb: e.dma_start(out=RG.ap, in_=rpbg[jb["j"]]), writes=[RG.reg()], sem="rg")
        P.add("dve", lambda e, bt=bt: e.tensor_tensor(out=bt.ap, in0=RG.ap, in1=MASKC.ap, op=ALU.add),
              reads=[RG.reg(), MASKC.reg()], writes=[bt.reg()])

    inproj_bank = [0]

    def inproj(n):
        jb = jobs[n]
        st, dil, kind = n % 2, jb["dil"], jb["kind"]
        stages = []
        for which in ("q", "k"):
            W = WQ[st] if which == "q" else WK[st]
            Wv_ = W.ap.rearrange("p (k c) -> p k c", c=128)
            dst = QT[st] if which == "q" else KT[st]
            if kind == "A":
                gvec = gqa if which == "q" else gka
            else:
                gvec = gqb if which == "q" else gkb
            for jbk in range(4):
                bcell = [None]

                def main_a(Wv_=Wv_, W=W, jbk=jbk, bcell=bcell):
                    b = inproj_bank[0] % 2
                    inproj_bank[0] += 1
                    bcell[0] = b
                    for k in range(8):
                        P.add("pe", lambda e, k=k, b=b: e.matmul(PS[b][:, :], lhsT=Wv_[:, k, :], rhs=XNv[:, k, jbk * 512:(jbk + 1) * 512],
                                                                  start=(k == 0), stop=False),
                              reads=[W.reg(k * 128, (k + 1) * 128), xn_reg(k, jbk * 512, (jbk + 1) * 512)], writes=[psr(b)])
                    return b

                def main(which=which, Wv_=Wv_, W=W, jbk=jbk, dil=dil, kind=kind, gvec=gvec, bcell=bcell):
                    b = bcell[0]
                    for k in range(8, 16):
                        P.add("pe", lambda e, k=k, b=b: e.matmul(PS[b][:, :], lhsT=Wv_[:, k, :], rhs=XNv[:, k, jbk * 512:(jbk + 1) * 512],
                                                                  start=False, stop=(k == 15)),
                              reads=[W.reg(k * 128, (k + 1) * 128), xn_reg(k, jbk * 512, (jbk + 1) * 512)], writes=[psr(b)])
                    sq = SQ[b]
                    P.add("act", lambda e, sq=sq, b=b: e.activation(out=sq.ap, in_=PS[b][:, :], func=AF.Square),
                          reads=[psr(b)], writes=[sq.reg()])
                    if kind == "A":
                        q1 = Q1[b]
                        P.add("act", lambda e, q1=q1, b=b: e.activation(out=q1.ap, in_=PS[b][:, :], func=AF.Copy, scale=gvec),
                              reads=[psr(b), RCF], writes=[q1.reg()])
                    return b

                def post(b, which=which, dst=dst, jbk=jbk, dil=dil, kind=kind, gvec=gvec):
                    sq, rs = SQ[b], RS[b]
                    P.add("pe", lambda e, sq=sq: e.matmul(PS[2][:, :], lhsT=ones128, rhs=sq.ap, start=True, stop=True),
                          reads=[sq.reg(), RCB], writes=[psr(2)])
                    if dil == 1:
                        dreg = dst.reg(jbk * 512, (jbk + 1) * 512)
                        dap = dst.ap[:, jbk * 512:(jbk + 1) * 512]
                        pv = lambda ap: ap
                    else:
                        Mq, mb = S // dil, 512 // dil
                        dreg = dst.reg(dj=("qtw%d" % dst.off, jbk))
                        dap = dst.ap.rearrange("p (r m) -> p r m", m=Mq)[:, :, mb * jbk:mb * (jbk + 1)]
                        pv = lambda ap: ap.rearrange("p (m r) -> p r m", r=dil)
                    nsl = slice(jbk * 512, (jbk + 1) * 512)
                    if kind == "A":
                        q1, t1, t2 = Q1[b], T1[b], T2[b]
                        P.add("pe", lambda e, q1=q1: e.matmul(PS[3][:, :], lhsT=permT, rhs=q1.ap, start=True, stop=True),
                              reads=[q1.reg(), RCB], writes=[psr(3)])
                        P.add("dve", lambda e, t1=t1, b=b: e.scalar_tensor_tensor(
                            out=t1.ap, in0=PS[b][:, :], scalar=gvec, in1=ROPEv[:, 0, nsl], op0=ALU.mult, op1=ALU.mult),
                            reads=[psr(b), ROPE.reg(jbk * 512, (jbk + 1) * 512), RCF], writes=[t1.reg()])
                        rstd_ops(rs, 2)
                        P.add("dve", lambda e, t2=t2: e.tensor_tensor(out=t2.ap, in0=PS[3][:, :], in1=ROPEv[:, 1, nsl], op=ALU.mult),
                              reads=[psr(3), ROPE.reg(S + jbk * 512, S + (jbk + 1) * 512)], writes=[t2.reg()])
                        P.add("dve", lambda e, t1=t1, t2=t2: e.tensor_tensor(out=t1.ap, in0=t1.ap, in1=t2.ap, op=ALU.add),
                              reads=[t1.reg(), t2.reg()], writes=[t1.reg()])
                        P.add("dve", lambda e, t1=t1, rs=rs: e.tensor_tensor(out=dap, in0=pv(t1.ap), in1=pv(rs.ap), op=ALU.mult),
                              reads=[t1.reg(), rs.reg()], writes=[dreg])
                    else:
                        rstd_ops(rs, 2)
                        P.add("dve", lambda e, rs=rs, b=b: e.scalar_tensor_tensor(out=dap, in0=PS[b][:, :], scalar=gvec, in1=rs.ap, op0=ALU.mult, op1=ALU.mult),
                              reads=[psr(b), rs.reg(), RCF], writes=[dreg])
                stages.append((main_a, lambda b: None))
                stages.append((main, post))
        Wvv = WV[st].ap.rearrange("p (k c) -> p k c", c=128)
        Vb = VV[st]
        for grp in range(4):
            vcell = [None]
            for ti in range(4):
                def mainv(grp=grp, ti=ti, Wvv=Wvv, Vb=Vb, dil=dil, st=st, vcell=vcell):
                    if ti == 0:
                        vcell[0] = inproj_bank[0] % 2
                        inproj_bank[0] += 1
                    b = vcell[0]
                    i = grp * 4 + ti
                    sl = tok_slice(dil, i)
                    for k in range(16):
                        P.add("pe", lambda e, k=k, b=b, ti=ti, sl=sl: e.matmul(PS[b][:, ti * 128:(ti + 1) * 128], lhsT=XNv[:, k, sl], rhs=Wvv[:, k, :],
                                                                          start=(k == 0), stop=(k == 15), skip_group_check=True),
                              reads=[WV[st].reg(k * 128, (k + 1) * 128), XN.reg(k * S + sl.start, k * S + min(sl.stop, S))], writes=[psr(b)])
                    if ti == 3:
                        P.add("act", lambda e, b=b, grp=grp: e.activation(out=Vb.ap[:, grp * 512:(grp + 1) * 512], in_=PS[b][:, :], func=AF.Copy),
                              reads=[psr(b)], writes=[Vb.reg(grp * 512, (grp + 1) * 512)])
                    return b
                stages.append((mainv, lambda b: None))
        return stages

    def steps_of(stages, skew=1):
        cells = [None] * len(stages)
        steps = []
        for i in range(len(stages) + skew):
            def f(i=i):
                if i < len(stages):
                    cells[i] = stages[i][0]()
                if 0 <= i - skew < len(stages):
                    stages[i - skew][1](cells[i - skew])
            steps.append(f)
        return steps

    def merge_run(a, b):
        na, nb = len(a), len(b)
        i = j = 0
        while i < na or j < nb:
            if j >= nb or (i < na and (i + 0.5) / na <= (j + 0.5) / nb):
                a[i]()
                i += 1
            else:
                b[j]()
                j += 1

    od_count = [0]

    def attention(n):
        jb = jobs[n]
        st, dil, kind, j, g = n % 2, jb["dil"], jb["kind"], jb["j"], jb["g"]
        qt, kt, vb = QT[st], KT[st], VV[st]
        M = S // dil
        nt = M // 128
        stages = []
        ucount = [0]
        qblocks = []
        if kind == "A":
            for T in range(16):
                qi = T % nt
                base = T - qi
                kts = [(s_, base + qi - 1 + s_) for s_ in range(3) if 0 <= qi - 1 + s_ < nt]
                qblocks.append(dict(T=T, units=[kts]))
        else:
            for pb in range(16):
                if pb < 2:
                    tiles, off = [0, 1, 2, 3], 2 * pb
                elif pb >= 14:
                    tiles, off = [12, 13, 14, 15], 4 + 2 * (pb - 14)
                else:
                    tiles, off = [pb - 2 + c for c in range(5)], 4
                lo = tiles[0]
                v = 0 if 2 <= pb < 14 else 1
                u1 = [(tl - lo) for tl in tiles[:4]][::-1]
                units = [[(s_, lo + c) for s_, c in enumerate(u1)]]
                dd0s = [6 - 2 * u1[0] + off]
                if len(tiles) == 5:
                    units.append([(0, lo + 4)])
                    dd0s.append(6 - 2 * 4 + off)
                qblocks.append(dict(T=pb, units=units, v=v, dd0s=dd0s))
        for qb in qblocks:
            T = qb["T"]
            odi = od_count[0]
            od_count[0] += 1
            obank = 6 + (odi // 2) % 2
            ocol = (odi % 2) * 256
            first_in_bank = (odi % 2 == 0)
            nun = len(qb["units"])
            for ui, unit in enumerate(qb["units"]):
                def s_stage(unit=unit, T=T, qb=qb, ui=ui):
                    u = ucount[0]
                    ucount[0] += 1
                    if kind == "A":
                        sb_, pt = 4 + u % 2, PT[u % 2]
                    else:
                        sb_, pt = 3 + u % 3, PT3[u % 3]
                    s0 = unit[0][0]
                    ns = len(unit)
                    for (s_, kti) in unit:
                        P.add("pe", lambda e, s_=s_, kti=kti, sb_=sb_: e.matmul(PS[sb_][:, s_ * 128:(s_ + 1) * 128], lhsT=kt.ap[:, kti * 128:(kti + 1) * 128],
                                                                            rhs=qt.ap[:, T * 128:(T + 1) * 128], start=True, stop=True, skip_group_check=True),
                              reads=[kt.reg(kti * 128, (kti + 1) * 128), qt.reg(T * 128, (T + 1) * 128)], writes=[psr(sb_)])
                    c0, c1 = s0 * 128, (s0 + ns) * 128
                    if kind == "A":
                        P.add("act", lambda e, sb_=sb_, pt=pt: e.activation(out=pt.ap[:, c0:c1], in_=PS[sb_][:, c0:c1], func=AF.Exp, scale=SCALE),
                              reads=[psr(sb_)], writes=[pt.reg(c0, c1)])
                        P.add("dve", lambda e, pt=pt: e.tensor_tensor(out=pt.ap[:, c0:c1], in0=pt.ap[:, c0:c1], in1=maskA[:, c0:c1], op=ALU.mult),
                              reads=[pt.reg(c0, c1), RCB], writes=[pt.reg(c0, c1)])
                    else:
                        bt, tmp = BT[n % 2], TMPB[u % 3]
                        v, dd0 = qb["v"], qb["dd0s"][ui]
                        b0 = v * 896 + dd0 * 64
                        P.add("dve", lambda e, sb_=sb_, tmp=tmp, bt=bt, b0=b0: e.scalar_tensor_tensor(
                            out=tmp.ap[:, c0:c1], in0=PS[sb_][:, c0:c1], scalar=SCALE, in1=bt.ap[:, b0:b0 + (c1 - c0)], op0=ALU.mult, op1=ALU.add),
                            reads=[psr(sb_), bt.reg()], writes=[tmp.reg(c0, c1)])
                        P.add("act", lambda e, tmp=tmp, pt=pt: e.activation(out=pt.ap[:, c0:c1], in_=tmp.ap[:, c0:c1], func=AF.Exp),
                              reads=[tmp.reg(c0, c1)], writes=[pt.reg(c0, c1)])
                    return (u, unit)

                def pv_stage(su, ui=ui, nun=nun, obank=obank, ocol=ocol, first_in_bank=first_in_bank, T=T, odi=odi):
                    u, unit = su
                    pt = PT[u % 2] if kind == "A" else PT3[u % 3]
                    nk = len(unit)
                    for ii, (s_, kti) in enumerate(unit):
                        P.add("pe", lambda e, s_=s_, kti=kti, ii=ii: e.matmul(PS[obank][:, ocol:ocol + 128], lhsT=vb.ap[:, kti * 128:(kti + 1) * 128],
                                                                        rhs=pt.ap[:, s_ * 128:(s_ + 1) * 128],
                                                                        start=(first_in_bank and ui == 0 and ii == 0), stop=(ui == nun - 1 and ii == nk - 1), skip_group_check=True),
                              reads=[vb.reg(kti * 128, (kti + 1) * 128), pt.reg(s_ * 128, (s_ + 1) * 128)], writes=[psr(obank)])
                    for ii, (s_, kti) in enumerate(unit):
                        P.add("pe", lambda e, s_=s_, ii=ii: e.matmul(PS[obank][:, ocol + 128:ocol + 256], lhsT=ones1, rhs=pt.ap[:, s_ * 128:(s_ + 1) * 128],
                                                              start=False, stop=(ui == nun - 1 and ii == nk - 1), skip_group_check=True),
                              reads=[pt.reg(s_ * 128, (s_ + 1) * 128), RCB], writes=[psr(obank)])
                    if ui == nun - 1 and odi % 2 == 1:
                        for (TT, oc) in ((T - 1, 0), (T, 256)):
                            evac(TT, obank, oc)

                stages.append((s_stage, pv_stage))

        def evac(TT, obank, oc):
            Ops = PS[obank][:, oc:oc + 128]
            Dps = PS[obank][:, oc + 128:oc + 256]
            if kind == "B":
                rc = RCB2[TT % 2]
                P.add("act", lambda e, rc=rc: e.activation(out=rc.ap, in_=Dps, func=AF.Ln), reads=[psr(obank)], writes=[rc.reg()])
                P.add("act", lambda e, rc=rc: e.activation(out=rc.ap, in_=rc.ap, func=AF.Exp, scale=-1.0), reads=[rc.reg()], writes=[rc.reg()])
                P.add("dve", lambda e, rc=rc: e.tensor_tensor(out=OABv[:, 4 + j, TT * 128:(TT + 1) * 128], in0=Ops, in1=rc.ap, op=ALU.mult),
                      reads=[psr(obank), rc.reg()], writes=[OAB.reg((4 + j) * S + TT * 128, (4 + j) * S + (TT + 1) * 128)])
                return
            sl = tok_slice(dil, TT)
            lo, hi = sl.start, sl.stop
            djk = ("acc%d" % dil, TT // nt) if dil > 1 else None
            accn = ACCN.ap[:, sl]
            accd = ACCD.ap[:, sl]
            rn, rd = ACCN.reg(lo, hi, djk), ACCD.reg(lo, hi, djk)
            if g == 0:
                P.add("act", lambda e: e.activation(out=accn, in_=Ops, func=AF.Copy), reads=[psr(obank)], writes=[rn])
                P.add("dve", lambda e: e.tensor_copy(out=accd, in_=Dps), reads=[psr(obank)], writes=[rd])
            elif g == 1:
                P.add("dve", lambda e: e.tensor_tensor(out=accn, in0=Ops, in1=accn, op=ALU.add), reads=[psr(obank), rn], writes=[rn])
                P.add("dve", lambda e: e.tensor_tensor(out=accd, in0=Dps, in1=accd, op=ALU.add), reads=[psr(obank), rd], writes=[rd])
            else:
                den, rc, num = DEN[TT % 2], RC[TT % 2], NUM[TT % 2]
                P.add("dve", lambda e: e.tensor_tensor(out=den.ap, in0=Dps, in1=accd, op=ALU.add), reads=[psr(obank), rd], writes=[den.reg()])
                P.add("dve", lambda e: e.tensor_tensor(out=num.ap, in0=Ops, in1=accn, op=ALU.add), reads=[psr(obank), rn], writes=[num.reg()])
                P.add("act", lambda e: e.activation(out=rc.ap, in_=den.ap, func=AF.Ln), reads=[den.reg()], writes=[rc.reg()])
                P.add("act", lambda e: e.activation(out=rc.ap, in_=rc.ap, func=AF.Exp, scale=-1.0), reads=[rc.reg()], writes=[rc.reg()])
                P.add("dve", lambda e: e.tensor_tensor(out=OABv[:, j, sl], in0=num.ap, in1=rc.ap, op=ALU.mult),
                      reads=[num.reg(), rc.reg()], writes=[OAB.reg(j * S + lo, j * S + hi, djk)])
        return stages

    P.add("sp", lambda e: e.dma_start(out=ROPEv[:, :, :], in_=roped[:, :, :]), writes=[ROPE.reg()], sem="rope")
    load_head_weights(0)
    load_head_weights(1)
    emit_precast(8, after=[xn_reg(15, 1536, 2048)])
    merge_run(steps_of(inproj(0)), [])
    for n in range(16):
        merge_run(steps_of(inproj(n + 1)) if n + 1 < 16 else [], steps_of(attention(n), skew=(2 if jobs[n]["kind"] == "B" else 1)))
        if n == 11:
            P.add("sp", lambda e: e.dma_start(out=MASKC.ap, in_=maskcd[:, :]), writes=[MASKC.reg()], sem="maskc")
        if n + 1 < 16 and jobs[n + 1]["kind"] == "B":
            setup_tables(n + 1)
        if n + 2 < 16:
            load_head_weights(n + 2)
        emit_precast(5)

    if debug:
        P.add("sp", lambda e: e.dma_start(out=dbg_xn[:, :], in_=XN.ap), reads=[XN.reg()], sem="dbg")
        P.add("sp", lambda e: e.dma_start(out=dbg_oab[:, :], in_=OAB.ap), reads=[OAB.reg()], sem="dbg")

    MIX = Buf(TB0, 65536, BF16)
    MIXv = MIX.ap.rearrange("p (tb k t) -> p tb k t", k=16, t=512)
    o = TB0 + 65536
    WGA = [Buf(o + i * 8192, 4096, BF16) for i in range(2)]
    WGB = [Buf(o + i * 8192 + 4096, 4096, BF16) for i in range(2)]
    o += 16384
    PA = [Buf(o + i * 2048, 1024, BF16) for i in range(2)]
    PBb = [Buf(o + i * 2048 + 1024, 1024, BF16) for i in range(2)]
    o += 4096
    SA = [Buf(o + i * 8192, 2048, F32) for i in range(2)]
    SBg = [Buf(o + i * 8192 + 2048, 2048, F32) for i in range(2)]
    M1 = [Buf(o + i * 8192 + 4096, 2048, F32) for i in range(2)]
    M2 = [Buf(o + i * 8192 + 6144, 2048, F32) for i in range(2)]
    w_pa_v = w_pa.rearrange("(s p) c -> p s c", p=128)
    w_pb_v = w_pb.rearrange("(s p) c -> p s c", p=128)

    def load_c1(c):
        st = c % 2
        P.add("pool", lambda e: e.dma_start(out=WGA[st].ap.rearrange("p (k c) -> p k c", c=128), in_=w_in_v[:, :, 6144 + c * 128:6144 + (c + 1) * 128]),
              writes=[WGA[st].reg()], sem="wg%d" % st)
        P.add("pool", lambda e: e.dma_start(out=WGB[st].ap.rearrange("p (k c) -> p k c", c=128), in_=w_in_v[:, :, 8192 + c * 128:8192 + (c + 1) * 128]),
              writes=[WGB[st].reg()], sem="wg%d" % st)
        P.add("pool", lambda e: e.dma_start(out=PA[st].ap.rearrange("p (s c) -> p s c", c=128), in_=w_pa_v[:, :, c * 128:(c + 1) * 128]),
              writes=[PA[st].reg()], sem="wg%d" % st)
        P.add("pool", lambda e: e.dma_start(out=PBb[st].ap.rearrange("p (s c) -> p s c", c=128), in_=w_pb_v[:, :, c * 128:(c + 1) * 128]),
              writes=[PBb[st].reg()], sem="wg%d" % st)

    load_c1(0)
    load_c1(1)
    cnt = 0
    for c in range(16):
        st = c % 2
        wga = WGA[st].ap.rearrange("p (k c) -> p k c", c=128)
        wgb = WGB[st].ap.rearrange("p (k c) -> p k c", c=128)
        pa = PA[st].ap.rearrange("p (s c) -> p s c", c=128)
        pbm = PBb[st].ap.rearrange("p (s c) -> p s c", c=128)
        for tb in range(4):
            bs = 4 * (cnt % 2)
            ss = cnt % 2
            cnt += 1
            tsl = slice(tb * 512, (tb + 1) * 512)
            for k in range(16):
                P.add("pe", lambda e, k=k, bs=bs, wga=wga, tsl=tsl: e.matmul(PS[bs][:, :], lhsT=wga[:, k, :], rhs=XNv[:, k, tsl], start=(k == 0), stop=(k == 15)),
                      reads=[WGA[st].reg(), xn_reg(k, tb * 512, (tb + 1) * 512)], writes=[psr(bs)])
            for k in range(16):
                P.add("pe", lambda e, k=k, bs=bs, wgb=wgb, tsl=tsl: e.matmul(PS[bs + 1][:, :], lhsT=wgb[:, k, :], rhs=XNv[:, k, tsl], start=(k == 0), stop=(k == 15)),
                      reads=[WGB[st].reg(), xn_reg(k, tb * 512, (tb + 1) * 512)], writes=[psr(bs + 1)])
            for s_ in range(4):
                P.add("pe", lambda e, s_=s_, bs=bs, pa=pa, tsl=tsl: e.matmul(PS[bs + 2][:, :], lhsT=pa[:, s_, :], rhs=OABv[:, s_, tsl], start=(s_ == 0), stop=(s_ == 3)),
                      reads=[PA[st].reg(), OAB.reg(s_ * S + tb * 512, s_ * S + (tb + 1) * 512)], writes=[psr(bs + 2)])
            for s_ in range(4):
                P.add("pe", lambda e, s_=s_, bs=bs, pbm=pbm, tsl=tsl: e.matmul(PS[bs + 3][:, :], lhsT=pbm[:, s_, :], rhs=OABv[:, 4 + s_, tsl], start=(s_ == 0), stop=(s_ == 3)),
                      reads=[PBb[st].reg(), OAB.reg((4 + s_) * S + tb * 512, (4 + s_) * S + (tb + 1) * 512)], writes=[psr(bs + 3)])
            sa, sb2, m1, m2 = SA[ss], SBg[ss], M1[ss], M2[ss]
            P.add("act", lambda e, sa=sa, bs=bs, c=c: e.activation(out=sa.ap, in_=PS[bs][:, :], func=AF.Sigmoid, bias=bgc(c), scale=1.0),
                  reads=[psr(bs), RCF], writes=[sa.reg()])
            P.add("act", lambda e, sb2=sb2, bs=bs, c=c: e.activation(out=sb2.ap, in_=PS[bs + 1][:, :], func=AF.Sigmoid, bias=bgc(16 + c), scale=1.0),
                  reads=[psr(bs + 1), RCF], writes=[sb2.reg()])
            P.add("dve", lambda e, m1=m1, sa=sa, bs=bs: e.tensor_tensor(out=m1.ap, in0=PS[bs + 2][:, :], in1=sa.ap, op=ALU.mult),
                  reads=[psr(bs + 2), sa.reg()], writes=[m1.reg()])
            P.add("dve", lambda e, m2=m2, sb2=sb2, bs=bs: e.tensor_tensor(out=m2.ap, in0=PS[bs + 3][:, :], in1=sb2.ap, op=ALU.mult),
                  reads=[psr(bs + 3), sb2.reg()], writes=[m2.reg()])
            mo = (tb * 16 + c) * 512
            P.add("dve", lambda e, m1=m1, m2=m2, tb=tb, c=c: e.tensor_tensor(out=MIXv[:, tb, c, :], in0=m1.ap, in1=m2.ap, op=ALU.add),
                  reads=[m1.reg(), m2.reg()], writes=[MIX.reg(mo, mo + 512)])
        if c + 2 < 16:
            load_c1(c + 2)
        emit_precast(4)
    emit_precast(1000)
    if debug:
        P.add("sp", lambda e: e.dma_start(out=dbg_mix[:, :], in_=MIX.ap), reads=[MIX.reg()], sem="dbg")

    U2 = Buf(0, 65536, BF16)
    U2v = U2.ap.rearrange("p (f t) -> p f t", t=512)
    Hb = Buf(65536, 32768, F32)
    Hv = Hb.ap.rearrange("p (c t) -> p c t", t=512)
    o = TB0 + 65536
    HN = Buf(o, 16384, BF16)
    HNv = HN.ap.rearrange("p (k t) -> p k t", t=512)
    o += 16384
    NRING = 4
    RING = [Buf(o + i * 4096, 4096, BF16) for i in range(NRING)]
    o += NRING * 4096
    TT_ = [Buf(o + i * 2048, 2048, F32) for i in range(2)]
    o += 4096
    SQD = [Buf(o + i * 1024, 1024, BF16) for i in range(2)]
    o += 2048
    R2 = Buf(o, 2048, F32)

    stream = []
    for tb in range(4):
        for c in range(16):
            stream.append((s_out[c], Reg("s_out", c, c + 1)))
        for f in range(64):
            stream.append((s_up[f], Reg("s_up", f, f + 1)))
        for cp in range(8):
            for fg in range(4):
                for c in (2 * cp, 2 * cp + 1):
                    t = c * 4 + fg
                    stream.append((s_dn[t], Reg("s_dn", t, t + 1)))
    spos = [0]
    PREF = 3

    def ring_load_upto(i):
        while spos[0] <= min(i, len(stream) - 1):
            n_ = spos[0]
            src, rg = stream[n_]
            rb = RING[n_ % NRING]
            P.add("sp", lambda e, src=src, rb=rb: e.dma_start(out=rb.ap, in_=src), reads=[rg], writes=[rb.reg()], sem="ring%d" % (n_ % NRING))
            spos[0] += 1

    tix = [0]

    def next_tile():
        n_ = tix[0]
        tix[0] += 1
        ring_load_upto(n_ + PREF)
        rb = RING[n_ % NRING]
        return rb, rb.ap.rearrange("p (k c) -> p k c", c=128)

    def xload(tb, c):
        P.add("sp", lambda e: e.dma_start(out=Hv[:, c, :], in_=xTv[:, c, tb * 512:(tb + 1) * 512]),
              writes=[Hb.reg(c * 512, (c + 1) * 512)], sem="hx")

    upbank = [0]
    outs = []
    for tb in range(4):
        xload(tb, 0)
        xload(tb, 1)
        pend = None
        for c in range(16):
            if c + 2 < 16:
                xload(tb, c + 2)
            rb, rv = next_tile()
            b = upbank[0] % 2
            upbank[0] += 1
            for k in range(16):
                P.add("pe", lambda e, k=k, b=b, rv=rv, tb=tb: e.matmul(PS[b][:, :], lhsT=rv[:, k, :], rhs=MIXv[:, tb, k, :], start=(k == 0), stop=(k == 15)),
                      reads=[rb.reg(), MIX.reg((tb * 16 + k) * 512, (tb * 16 + k + 1) * 512)], writes=[psr(b)])
            hreg = Hb.reg(c * 512, (c + 1) * 512)
            P.add("dve", lambda e, b=b, c=c: e.tensor_tensor(out=Hv[:, c, :], in0=PS[b][:, :], in1=Hv[:, c, :], op=ALU.add),
                  reads=[psr(b), hreg], writes=[hreg])
            P.add("act", lambda e, c=c: e.activation(out=HNv[:, c, :], in_=Hv[:, c, :], func=AF.Copy, scale=g2(c)),
                  reads=[hreg, RCF], writes=[HN.reg(c * 512, (c + 1) * 512)])
            sq = SQD[c % 2]
            P.add("act", lambda e, c=c, sq=sq: e.activation(out=sq.ap, in_=Hv[:, c, :], func=AF.Square),
                  reads=[hreg], writes=[sq.reg()])
            if pend is not None:
                pend()
            pend = (lambda sq=sq, c=c: P.add("pe", lambda e: e.matmul(PS[2][:, :], lhsT=ones2048, rhs=sq.ap, start=(c == 0), stop=(c == 15)),
                                             reads=[sq.reg(), RCB], writes=[psr(2)]))
        pend()
        rstd_ops(R2, 2)
        for f in range(64):
            rb, rv = next_tile()
            b = upbank[0] % 2
            upbank[0] += 1
            for k in range(16):
                P.add("pe", lambda e, k=k, b=b, rv=rv: e.matmul(PS[b][:, :], lhsT=rv[:, k, :], rhs=HNv[:, k, :], start=(k == 0), stop=(k == 15)),
                      reads=[rb.reg(), HN.reg(k * 512, (k + 1) * 512)], writes=[psr(b)])
            tt = TT_[f % 2]
            P.add("dve", lambda e, b=b, tt=tt: e.scalar_tensor_tensor(out=tt.ap, in0=PS[b][:, :], scalar=0.0, in1=R2.ap, op0=ALU.max, op1=ALU.mult),
                  reads=[psr(b), R2.reg()], writes=[tt.reg()])
            P.add("act", lambda e, f=f, tt=tt: e.activation(out=U2v[:, f, :], in_=tt.ap, func=AF.Square),
                  reads=[tt.reg()], writes=[U2.reg(f * 512, (f + 1) * 512)])
        for cp in range(8):
            banks = (3, 4) if cp % 2 == 0 else (5, 6)
            for fg in range(4):
                for ci, c in enumerate((2 * cp, 2 * cp + 1)):
                    rb, rv = next_tile()
                    bk = banks[ci]
                    for kk in range(16):
                        f = fg * 16 + kk
                        P.add("pe", lambda e, kk=kk, f=f, bk=bk, rv=rv, fg=fg: e.matmul(PS[bk][:, :], lhsT=rv[:, kk, :], rhs=U2v[:, f, :],
                                                                              start=(fg == 0 and kk == 0), stop=(fg == 3 and kk == 15)),
                              reads=[rb.reg(), U2.reg(f * 512, (f + 1) * 512)], writes=[psr(bk)])
            for ci, c in enumerate((2 * cp, 2 * cp + 1)):
                bk = banks[ci]
                hreg = Hb.reg(c * 512, (c + 1) * 512)
                P.add("dve", lambda e, bk=bk, c=c: e.tensor_tensor(out=Hv[:, c, :], in0=PS[bk][:, :], in1=Hv[:, c, :], op=ALU.add),
                      reads=[psr(bk), hreg], writes=[hreg])
                outs.append(P.add("sp", lambda e, c=c, tb=tb: e.dma_start(out=outTv[:, c, tb * 512:(tb + 1) * 512], in_=Hv[:, c, :]),
                                  reads=[hreg], sem="ost"))
    final_sems = ["ost"] + (["dbg"] if debug else [])

    from contextlib import ExitStack
    with ExitStack() as es:
        es.enter_context(nc.allow_low_precision("bf16 matmul operands, fp32 accumulation"))
        esems = {e: es.enter_context(nc.semaphore("e_" + e)) for e in ("pe", "act", "dve")}
        dsems = {name: es.enter_context(nc.semaphore("d_" + name)) for name in P.dsem}
        block = es.enter_context(nc.Block())
        run = P.emit(esems, dsems)

        def mk(ename):
            def f(eng):
                run(ename, eng)
                if ename == "sp":
                    for sname in final_sems:
                        eng.wait_ge(dsems[sname], P.dsem[sname])
            return f
        block.tensor(mk("pe"))
        block.scalar(mk("act"))
        block.vector(mk("dve"))
        block.gpsimd(mk("pool"))
        block.sync(mk("sp"))
    return nc


_CACHE = {}


def _prep_inputs(inp):
    cb, rope, maskc = _const_tables()
    f32 = np.float32
    cvec = np.zeros((128, 69), f32)
    cvec[:, 68] = EPS
    cvec[:, 0:16] = np.asarray(inp["norm_mix"], f32).reshape(16, 128).T
    cvec[:, 16:32] = np.asarray(inp["norm_ffn"], f32).reshape(16, 128).T
    cvec[:, 32:64] = np.asarray(inp["b_gate"], f32).reshape(32, 128).T
    cvec[:, 64] = np.asarray(inp["q_norm_a"], f32).reshape(128)
    cvec[:, 65] = np.asarray(inp["k_norm_a"], f32).reshape(128)
    cvec[:, 66] = np.asarray(inp["q_norm_b"], f32).reshape(128)
    cvec[:, 67] = np.asarray(inp["k_norm_b"], f32).reshape(128)
    shared = {
        "w_in": np.ascontiguousarray(np.asarray(inp["w_in"], f32)[0]),
        "w_pa": np.ascontiguousarray(np.asarray(inp["w_proj_a"], f32)[0]),
        "w_pb": np.ascontiguousarray(np.asarray(inp["w_proj_b"], f32)[0]),
        "w_out": np.ascontiguousarray(np.asarray(inp["w_out"], f32)[0]),
        "w_up": np.ascontiguousarray(np.asarray(inp["w_up"], f32)[0]),
        "w_dn": np.ascontiguousarray(np.asarray(inp["w_down"], f32)[0]),
        "cvec": cvec, "cb": cb, "rope": rope, "maskc": maskc,
        "rpbg": _rpb_gather(np.asarray(inp["rpb_b"], f32)[0]),
    }
    x = np.asarray(inp["x"], f32)
    in_maps = []
    for b in range(NCORES):
        m = dict(shared)
        m["xT"] = np.ascontiguousarray(x[b].T)
        in_maps.append(m)
    return in_maps


def kernel(**inputs):
    if "nc" not in _CACHE:
        _CACHE["nc"] = build(debug=False)
    nc = _CACHE["nc"]
    in_maps = _prep_inputs(inputs)
    res = run_bass_kernel_spmd(nc, in_maps, core_ids=list(range(NCORES)))
    out = np.stack([np.ascontiguousarray(res.results[b]["outT"].T) for b in range(NCORES)], axis=0)
    return out.astype(np.float32)
```
